# Optimizing a Trainium2 kernel written in Bass

```python
import math
import jax, jax.numpy as jnp
from jax import lax
import numpy as np

D_MODEL = 1024
BATCH = 8
SEQ = 2048
DEPTH = 2

GRID_W = 64
ROPE_THETA = 10000.0
HEAD_DIM = 64
ATT_Q_HEADS = 8
ATT_KV_HEADS = 2
Q_BLOCK = 128
ATT_Q_W = ATT_Q_HEADS * HEAD_DIM
ATT_KV_W = ATT_KV_HEADS * HEAD_DIM
HY_WIDTH = 512
HY_ORDER = 2
HY_BANDS = 16
HY_EMB = 1 + 2 * HY_BANDS
HY_FFN = 64
HY_FAST_DECAY = 0.3
HY_SLOW_DECAY = 1.5
HY_TARGET = 1e-2
GLA_HEADS = 4
GLA_DK = 64
GLA_DV = 128
GLA_RANK = 16
GLA_NORMALIZER = 16.0
GLA_CHUNK = 64
GLA_K_W = GLA_HEADS * GLA_DK
GLA_V_W = GLA_HEADS * GLA_DV
MEM_LEN = 256
X_HEADS = 4
X_HEAD_DIM = D_MODEL // X_HEADS
D_FF = 4 * D_MODEL
N_BRANCH = 3
EPS = 1e-6
IN_SPLITS = (ATT_Q_W, ATT_KV_W, ATT_KV_W, 3 * HY_WIDTH, GLA_K_W, GLA_K_W, GLA_V_W, GLA_V_W,
             2 * GLA_RANK, N_BRANCH * D_MODEL)
N_IN = ATT_Q_W + 2 * ATT_KV_W + 3 * HY_WIDTH + 2 * GLA_K_W + 2 * GLA_V_W + 2 * GLA_RANK + N_BRANCH * D_MODEL

kernel_name = "hybrid_gqa_hyena_gla_gated_encoder"

F32 = jnp.float32


def rmsnorm(x, g):
    xf = x.astype(F32)
    y = xf * lax.rsqrt(jnp.mean(xf * xf, axis=-1, keepdims=True) + EPS)
    return (y * g.astype(F32)).astype(x.dtype)


def split_cols(u, sizes):
    idx = [int(i) for i in np.cumsum(sizes)[:-1]]
    return jnp.split(u, idx, axis=-1)


def axial_rope_angles(L):
    rows = L // GRID_W
    r, c = jnp.meshgrid(jnp.arange(rows), jnp.arange(GRID_W), indexing="ij")
    n_freq = HEAD_DIM // 4
    inv = ROPE_THETA ** (-jnp.arange(n_freq, dtype=F32) / n_freq)
    pos = jnp.stack([r.reshape(-1), c.reshape(-1)], axis=1).astype(F32)
    ang = pos[:, :, None] * inv
    return jnp.cos(ang), jnp.sin(ang)


def apply_rope(x, cos, sin):
    xs = x.reshape(x.shape[:-1] + (2, 2, HEAD_DIM // 4)).astype(F32)
    x1, x2 = xs[..., 0, :], xs[..., 1, :]
    out = jnp.stack([x1 * cos - x2 * sin, x1 * sin + x2 * cos], axis=-2)
    return out.reshape(x.shape).astype(x.dtype)


def attention_mixer(q, k, v, qn, kn, cos, sin):
    B, L, _ = q.shape
    G = ATT_Q_HEADS // ATT_KV_HEADS
    q = rmsnorm(q.reshape(B, L, ATT_Q_HEADS, HEAD_DIM), qn).transpose(0, 2, 1, 3)
    k = rmsnorm(k.reshape(B, L, ATT_KV_HEADS, HEAD_DIM), kn).transpose(0, 2, 1, 3)
    v = v.reshape(B, L, ATT_KV_HEADS, HEAD_DIM).transpose(0, 2, 1, 3)
    q = apply_rope(q, cos, sin).reshape(B, ATT_KV_HEADS, G, L, HEAD_DIM)
    k = apply_rope(k, cos, sin)
    nb = L // Q_BLOCK
    qb = jnp.moveaxis(q.reshape(B, ATT_KV_HEADS, G, nb, Q_BLOCK, HEAD_DIM), 3, 0)
    scale = HEAD_DIM ** -0.5

    def block(qblk):
        s = jnp.einsum("bkgqd,bktd->bkgqt", qblk, k).astype(F32) * scale
        p = jax.nn.softmax(s, axis=-1).astype(v.dtype)
        return jnp.einsum("bkgqt,bktd->bkgqd", p, v)

    o = lax.map(block, qb)
    o = jnp.moveaxis(o, 0, 3).reshape(B, ATT_Q_HEADS, L, HEAD_DIM)
    return o.transpose(0, 2, 1, 3).reshape(B, L, ATT_Q_W)


def short_conv(u, w):
    up = jnp.pad(u, ((0, 0), (1, 1), (0, 0)))
    return up[:, :-2] * w[0] + up[:, 1:-1] * w[1] + up[:, 2:] * w[2]


def hyena_pos_features(L):
    t = jnp.arange(L, dtype=F32)
    t_norm = t / max(L - 1, 1)
    w = 2.0 * math.pi * t / L
    f = jnp.linspace(1e-4, HY_BANDS - 1, HY_BANDS, dtype=F32)
    fw = w[:, None] * f
    z = jnp.concatenate([t_norm[:, None], jnp.cos(fw), -jnp.sin(fw)], axis=-1)
    deltas = jnp.abs(jnp.linspace(math.log(HY_TARGET) / HY_FAST_DECAY,
                                  math.log(HY_TARGET) / HY_SLOW_DECAY, HY_WIDTH, dtype=F32))
    window = jnp.exp(-t_norm[:, None] * deltas)
    return z, window


def hyena_filter_spectra(z, window, w1, b1, w2, b2, w3, b3, freq):
    L = z.shape[0]
    c = lambda a: a.astype(F32)
    h = jnp.sin(c(freq[0]) * (z @ c(w1) + c(b1)))
    h = jnp.sin(c(freq[1]) * (h @ c(w2) + c(b2)))
    h = (h @ c(w3) + c(b3)).reshape(L, HY_ORDER, 2, HY_WIDTH) * window[:, None, None, :]
    fwd, bwd = h[:, :, 0], h[:, :, 1]
    f = jnp.concatenate([fwd, jnp.zeros((1, HY_ORDER, HY_WIDTH), F32), bwd[1:][::-1]], axis=0)
    f = f / (jnp.sum(jnp.abs(f), axis=0, keepdims=True) + EPS)
    return jnp.fft.rfft(f, axis=0)


def long_conv(u, spec, skip):
    L = u.shape[1]
    uf = u.astype(F32)
    y = jnp.fft.irfft(jnp.fft.rfft(uf, n=2 * L, axis=1) * spec, n=2 * L, axis=1)[:, :L]
    return (y + uf * skip.astype(F32)).astype(u.dtype)


def hyena_mixer(u, conv_w, spec, skip):
    u = short_conv(u, conv_w)
    v, x1, x2 = jnp.split(u, 3, axis=-1)
    z = x1 * long_conv(v, spec[:, 0], skip[0])
    return x2 * long_conv(z, spec[:, 1], skip[1])


def gla_scan(q, k, v, g, include_diag):
    B, H, L, dk = q.shape
    dv = v.shape[-1]
    C = GLA_CHUNK
    N = L // C
    q, k, g = (a.reshape(B, H, N, C, dk) for a in (q, k, g))
    v = v.reshape(B, H, N, C, dv)
    b = jnp.cumsum(g, axis=3)
    b_last = b[:, :, :, -1:, :]
    q_dec = q * jnp.exp(b)
    a = jnp.einsum("bhncd,bhnsd->bhncs", q_dec, k * jnp.exp(-b))
    mask = jnp.tril(jnp.ones((C, C), dtype=bool), k=0 if include_diag else -1)
    o = jnp.einsum("bhncs,bhnse->bhnce", jnp.where(mask, a, 0.0), v)
    d_state = jnp.einsum("bhncd,bhnce->bhnde", k * jnp.exp(b_last - b), v)
    decay = jnp.exp(b_last[:, :, :, 0, :])

    def step(S, inp):
        dec, ds = inp
        return dec[..., None] * S + ds, S

    _, s_prev = lax.scan(step, jnp.zeros((B, H, dk, dv), F32),
                         (jnp.moveaxis(decay, 2, 0), jnp.moveaxis(d_state, 2, 0)))
    s_prev = jnp.moveaxis(s_prev, 0, 2)
    o = o + jnp.einsum("bhncd,bhnde->bhnce", q_dec, s_prev)
    return o.reshape(B, H, L, dv)


def gla_mixer(q, k, v, og, lr, w_lr, b_lr, onorm):
    B, L, _ = q.shape
    heads = lambda a, d: a.astype(F32).reshape(B, L, GLA_HEADS, d).transpose(0, 2, 1, 3)
    q = heads(q, GLA_DK) * (GLA_DK ** -0.5)
    k = heads(k, GLA_DK)
    v = heads(v, GLA_DV)
    logit = jnp.einsum("btsr,srd->btsd", lr.astype(F32).reshape(B, L, 2, GLA_RANK),
                       w_lr.astype(F32)) + b_lr.astype(F32)
    g = jax.nn.log_sigmoid(logit) / GLA_NORMALIZER
    g_f = heads(g[:, :, 0], GLA_DK)
    g_b = heads(g[:, :, 1], GLA_DK)
    fl = lambda a: jnp.flip(a, axis=2)
    o_f = gla_scan(q, k, v, g_f, True)
    o_b = fl(gla_scan(fl(q), fl(k), fl(v), fl(g_b), False))
    o = (o_f + o_b).transpose(0, 2, 1, 3)
    o = rmsnorm(o, onorm) * jax.nn.silu(og.astype(F32).reshape(B, L, GLA_HEADS, GLA_DV))
    return o.reshape(B, L, GLA_V_W).astype(og.dtype)


def memory_cross_attention(hn, mn, wq, wk, wv, wo, qn, kn):
    B, L, _ = hn.shape
    M = mn.shape[1]
    q = rmsnorm((hn @ wq).reshape(B, L, X_HEADS, X_HEAD_DIM), qn)
    k = rmsnorm((mn @ wk).reshape(B, M, X_HEADS, X_HEAD_DIM), kn)
    v = (mn @ wv).reshape(B, M, X_HEADS, X_HEAD_DIM)
    s = jnp.einsum("blhd,bmhd->bhlm", q, k).astype(F32) * (X_HEAD_DIM ** -0.5)
    p = jax.nn.softmax(s, axis=-1).astype(v.dtype)
    o = jnp.einsum("bhlm,bmhd->blhd", p, v).reshape(B, L, D_MODEL)
    return o @ wo


def setup_inputs(seed: int = 0) -> dict:
    key = jax.random.key(seed)
    ks = iter(jax.random.split(key, 40))

    def nrm(shape, s):
        return jax.random.normal(next(ks), shape, F32) * s

    def gain(shape):
        return 1.0 + nrm(shape, 0.02)

    Ld = DEPTH
    return {
        "x": nrm((BATCH, SEQ, D_MODEL), 1.0),
        "mem": nrm((BATCH, MEM_LEN, D_MODEL), 1.0),
        "ln_mix": gain((Ld, D_MODEL)),
        "w_in": nrm((Ld, D_MODEL, N_IN), D_MODEL ** -0.5),
        "attn_qnorm": gain((Ld, HEAD_DIM)),
        "attn_knorm": gain((Ld, HEAD_DIM)),
        "hy_conv": nrm((Ld, 3, 3 * HY_WIDTH), 3 ** -0.5),
        "hy_w1": nrm((Ld, HY_EMB, HY_FFN), HY_EMB ** -0.5),
        "hy_b1": nrm((Ld, HY_FFN), 0.1),
        "hy_w2": nrm((Ld, HY_FFN, HY_FFN), HY_FFN ** -0.5),
        "hy_b2": nrm((Ld, HY_FFN), 0.1),
        "hy_w3": nrm((Ld, HY_FFN, HY_ORDER * 2 * HY_WIDTH), HY_FFN ** -0.5),
        "hy_b3": nrm((Ld, HY_ORDER * 2 * HY_WIDTH), 0.1),
        "hy_freq": 1.0 + nrm((Ld, 2, HY_FFN), 0.1),
        "hy_skip": nrm((Ld, HY_ORDER, HY_WIDTH), 0.5),
        "gla_w_lr": nrm((Ld, 2, GLA_RANK, GLA_K_W), GLA_RANK ** -0.5),
        "gla_b_lr": nrm((Ld, 2, GLA_K_W), 0.1),
        "gla_onorm": gain((Ld, GLA_DV)),
        "w_br_attn": nrm((Ld, ATT_Q_W, D_MODEL), ATT_Q_W ** -0.5),
        "w_br_hyena": nrm((Ld, HY_WIDTH, D_MODEL), HY_WIDTH ** -0.5),
        "w_br_gla": nrm((Ld, GLA_V_W, D_MODEL), GLA_V_W ** -0.5),
        "w_out": nrm((Ld, D_MODEL, D_MODEL), D_MODEL ** -0.5),
        "ln_x": gain((Ld, D_MODEL)),
        "ln_mem": gain((Ld, D_MODEL)),
        "x_wq": nrm((Ld, D_MODEL, D_MODEL), D_MODEL ** -0.5),
        "x_wk": nrm((Ld, D_MODEL, D_MODEL), D_MODEL ** -0.5),
        "x_wv": nrm((Ld, D_MODEL, D_MODEL), D_MODEL ** -0.5),
        "x_wo": nrm((Ld, D_MODEL, D_MODEL), D_MODEL ** -0.5),
        "x_qnorm": gain((Ld, X_HEAD_DIM)),
        "x_knorm": gain((Ld, X_HEAD_DIM)),
        "ln_mlp": gain((Ld, D_MODEL)),
        "mlp_w1": nrm((Ld, D_MODEL, D_FF), D_MODEL ** -0.5),
        "mlp_w2": nrm((Ld, D_FF, D_MODEL), D_FF ** -0.5),
    }


def reference(x, mem, ln_mix, w_in, attn_qnorm, attn_knorm, hy_conv, hy_w1, hy_b1, hy_w2, hy_b2,
              hy_w3, hy_b3, hy_freq, hy_skip, gla_w_lr, gla_b_lr, gla_onorm, w_br_attn, w_br_hyena,
              w_br_gla, w_out, ln_x, ln_mem, x_wq, x_wk, x_wv, x_wo, x_qnorm, x_knorm, ln_mlp,
              mlp_w1, mlp_w2):
    B, L, _ = x.shape
    cos, sin = axial_rope_angles(L)
    z_pos, window = hyena_pos_features(L)
    h = x
    for i in range(DEPTH):
        hn = rmsnorm(h, ln_mix[i])
        (aq, ak, av, hy, gq, gk, gv, go, glr, gates) = split_cols(hn @ w_in[i], IN_SPLITS)
        y_a = attention_mixer(aq, ak, av, attn_qnorm[i], attn_knorm[i], cos, sin) @ w_br_attn[i]
        spec = hyena_filter_spectra(z_pos, window, hy_w1[i], hy_b1[i], hy_w2[i], hy_b2[i],
                                    hy_w3[i], hy_b3[i], hy_freq[i])
        y_b = hyena_mixer(hy, hy_conv[i], spec, hy_skip[i]) @ w_br_hyena[i]
        y_c = gla_mixer(gq, gk, gv, go, glr, gla_w_lr[i], gla_b_lr[i], gla_onorm[i]) @ w_br_gla[i]
        gate = jax.nn.sigmoid(gates.reshape(B, L, N_BRANCH, D_MODEL))
        mixed = gate[:, :, 0] * y_a + gate[:, :, 1] * y_b + gate[:, :, 2] * y_c
        h = h + mixed @ w_out[i]
        h = h + memory_cross_attention(rmsnorm(h, ln_x[i]), rmsnorm(mem, ln_mem[i]),
                                       x_wq[i], x_wk[i], x_wv[i], x_wo[i], x_qnorm[i], x_knorm[i])
        hn = rmsnorm(h, ln_mlp[i])
        h = h + jnp.square(jax.nn.relu(hn @ mlp_w1[i])) @ mlp_w2[i]
    return h
```

```python
import math
from contextlib import ExitStack

import numpy as np
import ml_dtypes
import concourse.bass as bass
import concourse.mybir as mybir
from concourse.bass_utils import run_bass_kernel_spmd

F32 = mybir.dt.float32
BF16 = mybir.dt.bfloat16
AF = mybir.ActivationFunctionType
ALU = mybir.AluOpType
AX = mybir.AxisListType

ENGS = ("pe", "act", "dve", "pool", "sp")

D = 1024
T = 2048
NT = 16
DEPTH = 2
MEM = 256
N_IN = 6944
EPS = 1e-6
C_HY = 768
C_GQ = 2304
C_GATE = 3872
ARENA_W = 53000


class Sched:
    def __init__(self, nc):
        self.nc = nc
        self.ops = {e: [] for e in ENGS}
        self.cnt = {e: 0 for e in ENGS}
        self.seen = {e: {} for e in ENGS}
        self.lastw = {}
        self.readers = {}
        self.dma_sems = {}
        self.dma_names = {}
        self.sem_objs = {}

    def _deps(self, reads, writes):
        ev = []
        for k in reads:
            if k in self.lastw:
                ev.append(self.lastw[k] + (True,))
        for k in writes:
            if k in self.lastw:
                ev.append(self.lastw[k] + (False,))
            ev.extend(r + (False,) for r in self.readers.get(k, ()))
        return ev

    def _commit(self, event, reads, writes):
        for k in reads:
            self.readers.setdefault(k, []).append(event)
        for k in writes:
            self.lastw[k] = event
            self.readers[k] = []

    def _waits_for(self, eng, events):
        need = {}
        for evt in events:
            s, v = evt[0], evt[1]
            raw = evt[2] if len(evt) > 2 else True
            if s == eng and eng == "pe":
                continue
            if v > need.get(s, 0):
                need[s] = v
        out = []
        seen = self.seen[eng]
        for s, v in need.items():
            if seen.get(s, 0) >= v:
                continue
            seen[s] = v
            out.append((s, v))
        return out

    def op(self, eng, fn, reads=(), writes=()):
        events = self._deps(reads, writes)
        waits = self._waits_for(eng, events)
        self.cnt[eng] += 1
        event = (eng, self.cnt[eng])
        self.ops[eng].append((waits, fn, (eng, 1)))
        self._commit(event, reads, writes)
        return event

    def rec(self, eng, reads=(), writes=()):
        return _Rec(self, eng, list(reads), list(writes))

    def dma(self, q, out, in_, sem, reads=(), writes=(), **kw):
        events = self._deps(reads, writes)
        waits = self._waits_for(q, events)
        semname = self.dma_names.setdefault(sem, "dq%d" % len(self.dma_names))
        self.dma_sems[semname] = self.dma_sems.get(semname, 0) + 16
        event = (semname, self.dma_sems[semname])
        fn = lambda e, out=out, in_=in_, kw=kw: e.dma_start(out=out, in_=in_, **kw)
        self.ops[q].append((waits, fn, (semname, 16)))
        self._commit(event, reads, writes)
        return event

    def barrier(self, keep=()):
        skip = {self.lastw[k][0] for k in keep if k in self.lastw}
        events = [(e, self.cnt[e]) for e in ENGS if self.cnt[e] > 0]
        events += [(s, v) for s, v in self.dma_sems.items() if s not in skip]
        for e in ENGS:
            waits = self._waits_for(e, events)
            if waits:
                self.ops[e].append((waits, None, None))
        kept = {k: self.lastw[k] for k in keep if k in self.lastw}
        self.lastw = kept
        self.readers = {}

    def emit(self, stack):
        nc = self.nc
        for n in list(ENGS) + list(self.dma_sems.keys()):
            self.sem_objs[n] = stack.enter_context(nc.semaphore(n))
        block = stack.enter_context(nc.Block())
        sems = self.sem_objs

        def runner(engname):
            def body(eng):
                for waits, fn, inc in self.ops[engname]:
                    for s, v in waits:
                        eng.wait_ge(sems[s], v)
                    if fn is not None:
                        fn(eng).then_inc(sems[inc[0]], inc[1])
            return body

        block.tensor(runner("pe"))
        block.scalar(runner("act"))
        block.vector(runner("dve"))
        block.gpsimd(runner("pool"))
        block.sync(runner("sp"))


class _Rec:
    def __init__(self, S, eng, reads, writes):
        self.S, self.eng, self.reads, self.writes = S, eng, reads, writes

    def __getattr__(self, name):
        def f(*a, **k):
            return self.S.op(self.eng, lambda e: getattr(e, name)(*a, **k), self.reads, self.writes)
        return f


class Arena:
    def __init__(self, ap):
        self.ap = ap
        self.top = 0

    def f32(self, n):
        a = self.ap[:, self.top:self.top + n]
        self.top += n
        assert self.top <= ARENA_W, self.top
        return a

    def bf16(self, n):
        w = (n + 1) // 2
        a = self.ap[:, self.top:self.top + w].bitcast(BF16)
        self.top += w
        assert self.top <= ARENA_W, self.top
        return a


_CONSTS = None


def host_consts():
    global _CONSTS
    if _CONSTS is not None:
        return _CONSTS
    c = {}
    t = np.arange(T)
    inv = 10000.0 ** (-np.arange(16, dtype=np.float64) / 16)
    pos = np.stack([t // 64, t % 64], axis=1).astype(np.float64)
    ang = pos[:, :, None] * inv
    cs = np.concatenate([np.cos(ang).reshape(T, 32), np.sin(ang).reshape(T, 32)], axis=1)
    c["c_rope"] = np.ascontiguousarray(cs.reshape(NT, 128, 64).transpose(1, 0, 2)).astype(np.float32)
    tf = np.arange(T, dtype=np.float32)
    t_norm = tf / np.float32(T - 1)
    w = np.float32(2.0 * math.pi) * tf / np.float32(T)
    f = np.linspace(1e-4, 15, 16, dtype=np.float32)
    fw = w[:, None] * f
    z = np.concatenate([t_norm[:, None], np.cos(fw), -np.sin(fw)], axis=-1).astype(np.float32)
    c["c_zT"] = np.ascontiguousarray(z.T)
    deltas = np.abs(np.linspace(math.log(1e-2) / 0.3, math.log(1e-2) / 1.5, 512, dtype=np.float32))
    c["c_win"] = np.exp(-t_norm[:, None] * deltas).astype(np.float32)
    fidx = np.arange(2048, dtype=np.int64)
    ph = ((2 * fidx[:, None] + 1) * t[None, :].astype(np.int64)) % 8192
    angle = ph.astype(np.float64) * (2.0 * math.pi / 8192.0)
    Cm = np.cos(angle)
    Sm = np.sin(angle)
    ft = np.stack([Cm, Sm], axis=0).reshape(2, 16, 128, NT, 128)
    ft = ft.transpose(1, 0, 4, 3, 2)
    c["c_ft"] = np.ascontiguousarray(ft).astype(ml_dtypes.bfloat16)
    g = np.stack([Cm, Sm], axis=0).reshape(2, 16, 128, 4, 512) * (2.0 / 4096.0)
    g = g.transpose(3, 2, 0, 1, 4).reshape(4, 128, 32, 512)
    c["c_g"] = np.ascontiguousarray(g).astype(ml_dtypes.bfloat16)
    s_i = np.arange(128)[:, None]
    t_i = np.arange(128)[None, :]
    le = (s_i <= t_i).astype(np.float32)
    ge = (s_i >= t_i).astype(np.float32)
    gt = (s_i > t_i).astype(np.float32)
    lt = (s_i < t_i).astype(np.float32)
    ident = np.eye(128, dtype=np.float32)
    c["c_m"] = np.ascontiguousarray(np.stack([le, ge, -gt, -lt, gt, ident], axis=1)).astype(np.float32)
    _CONSTS = c
    return c


WEIGHT_SPECS = [
    ("ln_mix", (DEPTH, D)), ("w_in", (DEPTH, D, N_IN)), ("attn_qnorm", (DEPTH, 64)), ("attn_knorm", (DEPTH, 64)),
    ("hy_conv", (DEPTH, 3, 1536)), ("hy_w1", (DEPTH, 33, 64)), ("hy_b1", (DEPTH, 64)), ("hy_w2", (DEPTH, 64, 64)),
    ("hy_b2", (DEPTH, 64)), ("hy_w3", (DEPTH, 64, 2048)), ("hy_b3", (DEPTH, 2048)), ("hy_freq", (DEPTH, 2, 64)),
    ("hy_skip", (DEPTH, 2, 512)), ("gla_w_lr", (DEPTH, 2, 16, 256)), ("gla_b_lr", (DEPTH, 2, 256)),
    ("gla_onorm", (DEPTH, 128)), ("w_br_attn", (DEPTH, 512, D)), ("w_br_hyena", (DEPTH, 512, D)),
    ("w_br_gla", (DEPTH, 512, D)), ("w_out", (DEPTH, D, D)), ("ln_x", (DEPTH, D)), ("ln_mem", (DEPTH, D)),
    ("x_wq", (DEPTH, D, D)), ("x_wk", (DEPTH, D, D)), ("x_wv", (DEPTH, D, D)), ("x_wo", (DEPTH, D, D)),
    ("x_qnorm", (DEPTH, 256)), ("x_knorm", (DEPTH, 256)), ("ln_mlp", (DEPTH, D)),
    ("mlp_w1", (DEPTH, D, 4 * D)), ("mlp_w2", (DEPTH, 4 * D, D)),
]


def build(phases=("H", "A", "G", "M", "X", "F"), nlayers=DEPTH, dbg=()):
    nc = bass.Bass("TRN2", target_bir_lowering=False)
    dr = {}
    dr["x"] = nc.dram_tensor("x", [T, D], F32, kind="ExternalInput").ap()
    dr["mem"] = nc.dram_tensor("mem", [MEM, D], F32, kind="ExternalInput").ap()
    for name, shp in WEIGHT_SPECS:
        dr[name] = nc.dram_tensor(name, list(shp), F32, kind="ExternalInput").ap()
    hc = host_consts()
    for name, arr in hc.items():
        dt = BF16 if arr.dtype == ml_dtypes.bfloat16 else F32
        dr[name] = nc.dram_tensor(name, list(arr.shape), dt, kind="ExternalInput").ap()
    out = nc.dram_tensor("out", [T, D], F32, kind="ExternalOutput").ap()
    dbg_aps = {}
    for name, shp, dt in dbg:
        dbg_aps[name] = nc.dram_tensor(name, list(shp), dt, kind="ExternalOutput").ap()
    hbuf = nc.dram_tensor("hbuf", [T, D], F32, kind="Internal").ap()
    specD = nc.dram_tensor("specD", [2, 16, 2, 128, 512], F32, kind="Internal").ap()
    hyD = nc.dram_tensor("hyD", [8, 128, T], BF16, kind="Internal").ap()

    with ExitStack() as st:
        arena_t = st.enter_context(nc.sbuf_tensor("arena", [128, ARENA_W], F32))
        ps = [st.enter_context(nc.psum_tensor(f"ps{i}", [128, 512], F32)) for i in range(8)]
        psb = [p[:, :].bitcast(BF16) for p in ps]
        S = Sched(nc)
        A = Arena(arena_t)
        PSK = [f"ps{i}" for i in range(8)]
        PE = lambda r=(), w=(): S.rec("pe", r, w)
        ACT = lambda r=(), w=(): S.rec("act", r, w)
        DVE = lambda r=(), w=(): S.rec("dve", r, w)
        POOL = lambda r=(), w=(): S.rec("pool", r, w)

        def LD(dst, src, key, q="sp", reads=()):
            S.dma(q, dst, src, str(key), reads=reads, writes=[key])

        def ST(dst, src, key, dkey, q="sp"):
            S.dma(q, dst, src, "st_" + str(key), reads=[key], writes=[dkey])

        cm = A.f32(6 * 128).rearrange("p (a b) -> p a b", a=6)
        LD(cm, dr["c_m"], "cm")
        M_LE, M_GE, M_NGT, M_NLT, M_GT, IDF = (cm[:, i, :] for i in range(6))
        cmb = A.bf16(3 * 128).rearrange("p (a b) -> p a b", a=3)
        DVE(["cm"], ["cmb"]).tensor_copy(out=cmb[:, 0, :], in_=M_LE)
        DVE(["cm"], ["cmb"]).tensor_copy(out=cmb[:, 1, :], in_=M_GT)
        DVE(["cm"], ["cmb"]).tensor_copy(out=cmb[:, 2, :], in_=IDF)
        MB_LE, MB_GT, IDB = cmb[:, 0, :], cmb[:, 1, :], cmb[:, 2, :]
        ones_f = A.f32(128)
        DVE([], ["ones_f"]).memset(ones_f, 1.0)
        ones_b = A.bf16(128)
        DVE([], ["ones_b"]).memset(ones_b, 1.0)
        rope = A.f32(NT * 64).rearrange("p (i c) -> p i c", i=NT)
        LD(rope, dr["c_rope"], "rope")
        hnT = A.bf16(8 * T).rearrange("p (k t) -> p k t", k=8)
        HNBASE = A.top
        HT = A.bf16(4 * T).rearrange("p (m t) -> p m t", m=4)
        HBASE = A.top
        OT = A.bf16(4 * T).rearrange("p (m t) -> p m t", m=4)
        GT = A.bf16(4 * T).rearrange("p (m t) -> p m t", m=4)
        BASE = A.top
        S.barrier()
        HN_ALL = [("hnT", i) for i in range(NT)]

        def bc_load(dst, src_row, key):
            n = dst.shape[-1]
            LD(dst, src_row.unsqueeze(0).broadcast_to([128, n]), key)

        def load_w(dst, src, key):
            nk = dst.shape[1]
            for k in range(nk):
                S.dma("pool", dst[:, k, :], src[k * 128:(k + 1) * 128, :], str(key), writes=[key])

        def rstd_from_ss(ss, width, key):
            ACT([key], [key]).activation(out=ss, in_=ss, func=AF.Ln, scale=1.0 / width, bias=EPS)
            ACT([key], [key]).activation(out=ss, in_=ss, func=AF.Exp, scale=-0.5)

        class Ctx:
            pass

        def norm_setup(gain_row):
            n = Ctx()
            n.g = A.f32(D)
            bc_load(n.g, gain_row, "ng")
            n.junk = A.bf16(D)
            n.hn = [A.bf16(D), A.bf16(D)]
            n.ss = [A.f32(1), A.f32(1)]
            n.cnt = 0
            return n

        def norm_tile(n, h_ap, hkey, col0, dstT, dkey, psi):
            b = n.cnt % 2
            n.cnt += 1
            ss, hn = n.ss[b], n.hn[b]
            kss, khn = f"nss{b}", f"nhn{b}"
            DVE([hkey], ["njunk", kss]).scalar_tensor_tensor(out=n.junk, in0=h_ap, scalar=1.0, in1=h_ap, op0=ALU.mult,
                                                             op1=ALU.mult, accum_out=ss)
            rstd_from_ss(ss, float(D), kss)
            DVE([hkey, kss, "ng"], [khn]).scalar_tensor_tensor(out=hn, in0=h_ap, scalar=ss[:, 0:1], in1=n.g, op0=ALU.mult, op1=ALU.mult)
            for k in range(8):
                PE([khn, "cmb"], [PSK[psi]]).transpose(psb[psi][:, k * 128:(k + 1) * 128], hn[:, k * 128:(k + 1) * 128], IDB)
            ACT([PSK[psi]], [dkey]).activation(out=dstT[:, :, col0:col0 + 128],
                                               in_=psb[psi][:, 0:1024].rearrange("p (k t) -> p k t", k=8), func=AF.Copy)

        def norm_A(n, h_ap, hkey, i):
            b = i % 2
            DVE([hkey], ["njunk", f"nss{b}"]).scalar_tensor_tensor(out=n.junk, in0=h_ap, scalar=1.0, in1=h_ap, op0=ALU.mult,
                                                                   op1=ALU.mult, accum_out=n.ss[b])
            rstd_from_ss(n.ss[b], float(D), f"nss{b}")

        def norm_B(n, h_ap, hkey, i, dstT, dkey, psi):
            b = i % 2
            ss, hn = n.ss[b], n.hn[b]
            kss, khn = f"nss{b}", f"nhn{b}"
            DVE([hkey, kss, "ng"], [khn]).scalar_tensor_tensor(out=hn, in0=h_ap, scalar=ss[:, 0:1], in1=n.g, op0=ALU.mult, op1=ALU.mult)
            for k in range(8):
                PE([khn, "cmb"], [PSK[psi]]).transpose(psb[psi][:, k * 128:(k + 1) * 128], hn[:, k * 128:(k + 1) * 128], IDB)
            ACT([PSK[psi]], [dkey]).activation(out=dstT[:, :, i * 128:(i + 1) * 128],
                                               in_=psb[psi][:, 0:1024].rearrange("p (k t) -> p k t", k=8), func=AF.Copy)

        def headnorm(src, nh, hd, gains, dst, key_src, key_dst, tmp_sq, ss, kss):
            v3 = lambda a: a.rearrange("p (h d) -> p h d", h=nh)
            DVE([key_src], ["hsq"]).tensor_tensor(out=tmp_sq, in0=src, in1=src, op=ALU.mult)
            DVE(["hsq"], [kss]).tensor_reduce(out=ss, in_=v3(tmp_sq), axis=AX.X, op=ALU.add)
            rstd_from_ss(ss, float(hd), kss)
            DVE([key_src, kss], ["hsq"]).tensor_tensor(out=v3(tmp_sq), in0=v3(src), in1=ss.unsqueeze(2).broadcast_to([128, nh, hd]), op=ALU.mult)
            DVE(["hsq", "gains"], [key_dst]).tensor_tensor(out=dst, in0=tmp_sq, in1=gains, op=ALU.mult)

        def phase_N1():
            A.top = BASE
            n = norm_setup(dr["ln_mix"][0])
            hb = [A.f32(D), A.f32(D)]
            for i in range(NT):
                b = i % 2
                LD(hb[b], dr["x"][i * 128:(i + 1) * 128, :], f"hb{b}")
                norm_tile(n, hb[b], f"hb{b}", i * 128, hnT, ("hnT", i), 6 + b)
            S.barrier()

        def phase_A(l):
            w_in = dr["w_in"][l]
            A.top = BASE
            wqkv = A.bf16(8 * 768).rearrange("p (k n) -> p k n", k=8)
            for g in range(2):
                for k in range(8):
                    S.dma("pool", wqkv[:, k, 0:512].rearrange("p (h g d) -> p h g d", h=4, g=2)[:, :, g, :],
                          w_in[k * 128:(k + 1) * 128, 0:512].rearrange("p (g h d) -> p g h d", g=2, h=4)[:, g, :, :],
                          "wqkv", writes=["wqkv"])
            load_w(wqkv[:, :, 512:768], w_in[:, 512:768], "wqkv")
            gains = A.f32(640)
            gtmp = A.f32(128)
            bc_load(gtmp[:, 0:64], dr["attn_qnorm"][l], "gtmp")
            bc_load(gtmp[:, 64:128], dr["attn_knorm"][l], "gtmp")
            DVE(["gtmp"], ["gains"]).tensor_scalar(out=gains[:, 0:512].rearrange("p (h d) -> p h d", h=8),
                                                   in0=gtmp[:, 0:64].unsqueeze(1).broadcast_to([128, 8, 64]),
                                                   scalar1=0.125, scalar2=None, op0=ALU.mult)
            DVE(["gtmp"], ["gains"]).tensor_copy(out=gains[:, 512:640].rearrange("p (h d) -> p h d", h=2),
                                                 in_=gtmp[:, 64:128].unsqueeze(1).broadcast_to([128, 2, 64]))
            qT = A.bf16(4 * T).rearrange("p (m t) -> p m t", m=4)
            kTm = A.bf16(2 * T).rearrange("p (g t) -> p g t", g=2)
            POOL([], ["kT"]).memset(kTm, 0.0)
            Vaug = A.bf16(NT * 256).rearrange("p (i g c) -> p i g c", i=NT, g=2)
            POOL([], ["Vaug"]).memset(Vaug, 1.0)
            CORE_TOP = A.top
            qk = [A.f32(640), A.f32(640)]
            sq = A.f32(640)
            ssq = [A.f32(10), A.f32(10)]
            t1 = [A.f32(640), A.f32(640)]
            ra, rb_, rc, rd = A.f32(320), A.f32(320), A.f32(320), A.f32(320)
            rot = [A.bf16(640), A.bf16(640)]
            v4 = lambda a: a.rearrange("p (h a d) -> p h a d", h=10, a=2)
            WG_TOP = ARENA_W - 12288
            assert A.top <= WG_TOP, A.top

            def a1_mm(i):
                b = i % 2
                pa, pb = PSK[2 * b], PSK[2 * b + 1]
                tok = slice(i * 128, (i + 1) * 128)
                for k in range(8):
                    PE([("hnT", i), "wqkv"], [pa]).matmul(ps[2 * b][:, 0:512], hnT[:, k, tok], wqkv[:, k, 0:512], start=(k == 0), stop=(k == 7))
                for k in range(8):
                    PE([("hnT", i), "wqkv"], [pb]).matmul(ps[2 * b + 1][:, 0:256], hnT[:, k, tok], wqkv[:, k, 512:768], start=(k == 0), stop=(k == 7))

            v3h = lambda a: a.rearrange("p (h d) -> p h d", h=10)

            def a1_stA(i):
                b = i % 2
                pa, pb = PSK[2 * b], PSK[2 * b + 1]
                qkb, kq = qk[b], f"qk{b}"
                ACT([pa], [kq]).activation(out=qkb[:, 0:512], in_=ps[2 * b][:, 0:512], func=AF.Copy)
                ACT([pb], [kq]).activation(out=qkb[:, 512:640], in_=ps[2 * b + 1][:, 0:128], func=AF.Copy)
                ACT([pb], ["Vaug"]).activation(out=Vaug[:, i, :, 0:64], in_=ps[2 * b + 1][:, 128:256].rearrange("p (g c) -> p g c", g=2), func=AF.Copy)
                DVE([kq], ["hsq"]).tensor_tensor(out=sq, in0=qkb, in1=qkb, op=ALU.mult)
                DVE(["hsq"], [f"ssq{b}"]).tensor_reduce(out=ssq[b], in_=v3h(sq), axis=AX.X, op=ALU.add)
                rstd_from_ss(ssq[b], 64.0, f"ssq{b}")

            def a1_stB(i):
                b = i % 2
                tok = slice(i * 128, (i + 1) * 128)
                qkb, kq = qk[b], f"qk{b}"
                t1b, kt1 = t1[b], f"t1{b}"
                DVE([kq, f"ssq{b}"], [kt1]).tensor_tensor(out=v3h(t1b), in0=v3h(qkb), in1=ssq[b].unsqueeze(2).broadcast_to([128, 10, 64]), op=ALU.mult)
                DVE([kt1, "gains"], [kt1]).tensor_tensor(out=t1b, in0=t1b, in1=gains, op=ALU.mult)
                tv = t1b.rearrange("p (h a b d) -> p h a b d", h=10, a=2, b=2)
                x1, x2 = tv[:, :, :, 0, :], tv[:, :, :, 1, :]
                cosb = rope[:, i, 0:32].rearrange("p (a d) -> p a d", a=2).unsqueeze(1).broadcast_to([128, 10, 2, 16])
                sinb = rope[:, i, 32:64].rearrange("p (a d) -> p a d", a=2).unsqueeze(1).broadcast_to([128, 10, 2, 16])
                rv = rot[b].rearrange("p (h a b d) -> p h a b d", h=10, a=2, b=2)
                krot = f"rot{b}"
                DVE([kt1, "rope"], ["ra"]).tensor_tensor(out=v4(ra), in0=x1, in1=cosb, op=ALU.mult)
                DVE([kt1, "rope"], ["rb"]).tensor_tensor(out=v4(rb_), in0=x2, in1=sinb, op=ALU.mult)
                DVE(["ra", "rb"], [krot]).tensor_tensor(out=rv[:, :, :, 0, :], in0=v4(ra), in1=v4(rb_), op=ALU.subtract)
                POOL([kt1, "rope"], ["rc"]).tensor_tensor(out=v4(rc), in0=x1, in1=sinb, op=ALU.mult)
                POOL([kt1, "rope"], ["rd"]).tensor_tensor(out=v4(rd), in0=x2, in1=cosb, op=ALU.mult)
                POOL(["rc", "rd"], [krot]).tensor_tensor(out=rv[:, :, :, 1, :], in0=v4(rc), in1=v4(rd), op=ALU.add)
                pt = 4 + b
                for j in range(5):
                    PE([krot, "cmb"], [PSK[pt]]).transpose(psb[pt][:, j * 128:(j + 1) * 128], rot[b][:, j * 128:(j + 1) * 128], IDB)
                ACT([PSK[pt]], ["qT"]).activation(out=qT[:, :, tok], in_=psb[pt][:, 0:512].rearrange("p (m t) -> p m t", m=4), func=AF.Copy)
                ACT([PSK[pt]], ["kT"]).activation(out=kTm[0:64, 0, tok], in_=psb[pt][0:64, 512:640], func=AF.Copy)
                ACT([PSK[pt]], ["kT"]).activation(out=kTm[64:128, 1, tok], in_=psb[pt][64:128, 512:640], func=AF.Copy)

            a1_mm(0)
            a1_stA(0)
            a1_mm(1)
            for i in range(NT):
                if i + 1 < NT:
                    a1_stA(i + 1)
                if i + 2 < NT:
                    a1_mm(i + 2)
                a1_stB(i)
            S.barrier()
            if "M" in phases:
                wgall = arena_t[:, WG_TOP:ARENA_W].bitcast(BF16).rearrange("p (k n) -> p k n", k=8)
                for k in range(8):
                    for hf in range(2):
                        S.dma("pool", wgall[:, k, hf * 1536:(hf + 1) * 1536], w_in[k * 128:(k + 1) * 128, C_GATE + hf * 1536:C_GATE + (hf + 1) * 1536], "wgall", writes=["wgall"])
            A.top = CORE_TOP
            pT = [A.bf16(512) for _ in range(4)]
            rsb = [A.f32(512), A.f32(512)]
            steps = [(m, g, qb, kt) for m in range(4) for g in range(2) for qb in range(4) for kt in range(NT)]
            LOOK = 3

            def s_mm(s):
                m, g, qb, kt = steps[s]
                pr = slice(g * 64, (g + 1) * 64)
                bk = s % 4
                PE(["qT", "kT"], [PSK[bk]]).matmul(ps[bk][:, :], kTm[:, g, kt * 128:(kt + 1) * 128], qT[:, m, qb * 512:(qb + 1) * 512], start=True, stop=True)

            for s in range(LOOK):
                s_mm(s)
            for s in range(len(steps)):
                m, g, qb, kt = steps[s]
                it = s // NT
                bk = s % 4
                po = 4 + (it % 2)
                ACT([PSK[bk]], [f"pT{bk}"]).activation(out=pT[bk], in_=ps[bk][:, :], func=AF.Exp)
                if s + LOOK < len(steps):
                    s_mm(s + LOOK)
                PE([f"pT{bk}", "Vaug"], [PSK[po]]).matmul(ps[po][:, :], Vaug[:, kt, g, :], pT[bk], start=(kt == 0), stop=(kt == NT - 1))
                if kt == NT - 1:
                    pr = slice(g * 64, (g + 1) * 64)
                    qs = slice(qb * 512, (qb + 1) * 512)
                    rs = rsb[it % 2]
                    krs = f"rsb{it % 2}"
                    DVE([PSK[po]], [krs]).reciprocal(out=rs[0:64, :], in_=ps[po][64:128, :])
                    DVE([PSK[po], krs], ["OT"]).tensor_tensor(out=OT[pr, m, qs], in0=ps[po][0:64, :], in1=rs[0:64, :], op=ALU.mult)
            S.barrier(keep=["wgall"])

        def phase_G(l):
            w_in = dr["w_in"][l]
            A.top = BASE
            wg = A.bf16(8 * 1568).rearrange("p (k n) -> p k n", k=8)
            load_w(wg, w_in[:, C_GQ:C_GQ + 1568], "wg")
            glrT = A.f32(T)
            wblk = A.f32(512)
            DVE([], ["wblk"]).memset(wblk[0:64, :], 0.0)
            LD(wblk[0:16, 0:256], dr["gla_w_lr"][l, 0], "wblk")
            LD(wblk[16:32, 256:512], dr["gla_w_lr"][l, 1], "wblk")
            LD(wblk[32:33, :], dr["gla_b_lr"][l].rearrange("s d -> (s d)").unsqueeze(0), "wblk")
            DVE([], ["glrT"]).memset(glrT[32:64, :], 1.0)
            for tb in range(4):
                for k in range(8):
                    PE(HN_ALL + ["wg"], [PSK[tb]]).matmul(ps[tb][0:32, :], wg[:, k, 1536:1568], hnT[:, k, tb * 512:(tb + 1) * 512], start=(k == 0), stop=(k == 7))
                ACT([PSK[tb]], ["glrT"]).activation(out=glrT[0:32, tb * 512:(tb + 1) * 512], in_=ps[tb][0:32, :], func=AF.Copy)
            onb = A.f32(512)
            bc_load(onb[:, 0:128], dr["gla_onorm"][l], "onb0")
            DVE(["onb0"], ["gains"]).tensor_copy(out=onb[:, 128:512].rearrange("p (h d) -> p h d", h=3), in_=onb[:, 0:128].unsqueeze(1).broadcast_to([128, 3, 128]))
            osb = A.f32(NT * 512).rearrange("p (i c) -> p i c", i=NT)
            qdT = [A.bf16(256).rearrange("p (c t) -> p c t", c=2) for _ in range(2)]
            kiT = [A.bf16(256).rearrange("p (c t) -> p c t", c=2) for _ in range(2)]
            ke = [A.bf16(256) for _ in range(2)]
            vs = [A.bf16(512) for _ in range(2)]
            decs = [A.f32(4) for _ in range(2)]
            ee = A.f32(256)
            sp_ = A.f32(256)
            Ex = [A.f32(256) for _ in range(3)]
            stg = A.bf16(512)
            Sst = A.f32(256).rearrange("p (c e) -> p c e", c=2)
            Sbm = A.bf16(512).rearrange("p (c h e) -> p c h e", c=2, h=2)
            Am = A.bf16(512).rearrange("p (h t) -> p h t", h=4)
            MC = [(M_LE, M_NGT), (M_GE, M_NLT)]
            S.barrier()

            def g_tile(i, d, n):
                b = n % 2
                tok = slice(i * 128, (i + 1) * 128)
                for (bank, c0) in ((0, 0), (1, 512)):
                    for k in range(8):
                        PE([("hnT", i), "wg"], [PSK[bank]]).matmul(ps[bank][:, :], hnT[:, k, tok], wg[:, k, c0:c0 + 512], start=(k == 0), stop=(k == 7))
                PE(["glrT", "wblk"], [PSK[7]]).matmul(ps[7][:, 0:256], glrT[0:33, tok], wblk[0:33, d * 256:(d + 1) * 256], start=True, stop=True)
                ACT([PSK[7]], ["ee"]).activation(out=ee, in_=ps[7][:, 0:256], func=AF.Exp, scale=-1.0)
                ACT(["ee"], ["sp"]).activation(out=sp_, in_=ee, func=AF.Ln, bias=1.0, scale=1.0)
                ACT([PSK[1]], [f"vs{b}"]).activation(out=vs[b], in_=ps[1][:, :], func=AF.Copy)
                PE(["sp", "cm"], [PSK[4]]).matmul(ps[4][:, 0:256], MC[d][0], sp_, start=True, stop=True)
                PE(["sp", "cm"], [PSK[4]]).matmul(ps[4][:, 256:512], MC[d][1], sp_, start=True, stop=True)
                for c in range(2):
                    PE(["sp", "ones_f"], [PSK[3]]).matmul(ps[3][:, 256 + c * 2:256 + c * 2 + 2], sp_[:, c * 128:(c + 1) * 128], ones_f[:, 0:2], start=True, stop=True)
                ACT([PSK[3]], [f"decs{b}"]).activation(out=decs[b], in_=ps[3][:, 256:260], func=AF.Exp, scale=-1.0 / 16)
                E1, E2, E3 = Ex
                ACT([PSK[4]], ["E0"]).activation(out=E1, in_=ps[4][:, 0:256], func=AF.Exp, scale=-1.0 / 16)
                ACT([PSK[4]], ["E1"]).activation(out=E2, in_=ps[4][:, 0:256], func=AF.Exp, scale=1.0 / 16)
                ACT([PSK[4]], ["E2"]).activation(out=E3, in_=ps[4][:, 256:512], func=AF.Exp, scale=1.0 / 16)
                DVE(["E0", PSK[0]], ["stg"]).scalar_tensor_tensor(out=stg[:, 0:256], in0=E1, scalar=0.125, in1=ps[0][:, 0:256], op0=ALU.mult, op1=ALU.mult)
                DVE(["E1", PSK[0]], ["stg"]).tensor_tensor(out=stg[:, 256:512], in0=E2, in1=ps[0][:, 256:512], op=ALU.mult)
                DVE(["E2", PSK[0]], [f"ke{b}"]).tensor_tensor(out=ke[b], in0=E3, in1=ps[0][:, 256:512], op=ALU.mult)
                for j in range(4):
                    PE(["stg", "cmb"], [PSK[7]]).transpose(psb[7][:, 512 + j * 128:512 + (j + 1) * 128], stg[:, j * 128:(j + 1) * 128], IDB)
                pv = psb[7][:, 512:1024].rearrange("p (a c t) -> p a c t", a=2, c=2)
                ACT([PSK[7]], [f"qdT{b}"]).activation(out=qdT[b], in_=pv[:, 0, :, :], func=AF.Copy)
                ACT([PSK[7]], [f"kiT{b}"]).activation(out=kiT[b], in_=pv[:, 1, :, :], func=AF.Copy)

            def scan_tile(d, i, n, first):
                b = n % 2
                if first:
                    DVE([], ["Sst"]).memset(Sst, 0.0)
                    DVE([], ["Sb"]).memset(Sbm, 0.0)
                for h in range(4):
                    c, hh = h // 2, h % 2
                    pr = slice(hh * 64, (hh + 1) * 64)
                    bk = 5 if hh == 0 else 3
                    PE([f"qdT{b}", f"kiT{b}"], [PSK[bk]]).matmul(ps[bk][:, c * 128:(c + 1) * 128], kiT[b][pr, c, :], qdT[b][pr, c, :], start=True, stop=True)
                mask = MB_LE if d == 0 else MB_GT
                Amv = Am.rearrange("p (c h) t -> p c h t", c=2)
                for hh in range(2):
                    bk = 5 if hh == 0 else 3
                    DVE([PSK[bk], "cmb"], ["Am"]).tensor_tensor(out=Amv[:, :, hh, :], in0=ps[bk][:, 0:256].rearrange("p (c t) -> p c t", c=2),
                                                                in1=mask.unsqueeze(1).broadcast_to([128, 2, 128]), op=ALU.mult)
                for h in range(4):
                    c, hh = h // 2, h % 2
                    PE(["Am", f"vs{b}"], [PSK[2]]).matmul(ps[2][:, h * 128:(h + 1) * 128], Am[:, h, :], vs[b][:, h * 128:(h + 1) * 128], start=True, stop=False)
                    PE([f"qdT{b}", "Sb"], [PSK[2]]).matmul(ps[2][:, h * 128:(h + 1) * 128], qdT[b][:, c, :], Sbm[:, c, hh, :], start=False, stop=True)
                if d == 0:
                    ACT([PSK[2]], [("osb", i)]).activation(out=osb[:, i, :], in_=ps[2][:, :], func=AF.Copy)
                else:
                    DVE([PSK[2], ("osb", i)], [("osb", i)]).tensor_tensor(out=osb[:, i, :], in0=ps[2][:, :], in1=osb[:, i, :], op=ALU.add)
                for c in range(2):
                    PE([f"ke{b}", f"vs{b}"], [PSK[6]]).matmul(ps[6][:, c * 256:(c + 1) * 256], ke[b][:, c * 128:(c + 1) * 128], vs[b][:, c * 256:(c + 1) * 256], start=True, stop=True)
                for c in range(2):
                    for hh in range(2):
                        pr = slice(hh * 64, (hh + 1) * 64)
                        DVE(["Sst", f"decs{b}", PSK[6]], ["Sst"]).scalar_tensor_tensor(out=Sst[pr, c, :], in0=Sst[pr, c, :], scalar=decs[b][pr, 2 * c:2 * c + 1],
                                                                                       in1=ps[6][pr, c * 256 + hh * 128:c * 256 + (hh + 1) * 128], op0=ALU.mult, op1=ALU.add)
                for hh in range(2):
                    pr = slice(hh * 64, (hh + 1) * 64)
                    ACT(["Sst"], ["Sb"]).activation(out=Sbm[pr, :, hh, :], in_=Sst[pr, :, :], func=AF.Copy)

            seq = [(0, i) for i in range(NT)] + [(1, i) for i in range(NT - 1, -1, -1)]
            g_tile(seq[0][1], seq[0][0], 0)
            for n, (d, i) in enumerate(seq):
                if n + 1 < len(seq):
                    g_tile(seq[n + 1][1], seq[n + 1][0], n + 1)
                scan_tile(d, i, n, first=(i == (0 if d == 0 else NT - 1)))
            S.barrier()
            sq = A.f32(512)
            ss4 = [A.f32(4), A.f32(4)]
            yb = [A.bf16(512), A.bf16(512)]
            sgt = [A.bf16(512), A.bf16(512)]

            def fin_mm(i):
                b = i % 2
                for k in range(8):
                    PE([("hnT", i), "wg"], [PSK[b]]).matmul(ps[b][:, :], hnT[:, k, i * 128:(i + 1) * 128], wg[:, k, 1024:1536], start=(k == 0), stop=(k == 7))

            def fin_tile(i):
                b = i % 2
                ACT([PSK[b]], [f"sgt{b}"]).activation(out=sgt[b], in_=ps[b][:, :], func=AF.Silu)
                o = osb[:, i, :]
                DVE([("osb", i)], ["sq"]).tensor_tensor(out=sq, in0=o, in1=o, op=ALU.mult)
                DVE(["sq"], [f"ss4{b}"]).tensor_reduce(out=ss4[b], in_=sq.rearrange("p (h d) -> p h d", h=4), axis=AX.X, op=ALU.add)
                rstd_from_ss(ss4[b], 128.0, f"ss4{b}")
                DVE([("osb", i), f"ss4{b}"], ["sq"]).tensor_tensor(out=sq.rearrange("p (h d) -> p h d", h=4), in0=o.rearrange("p (h d) -> p h d", h=4),
                                                                   in1=ss4[b].unsqueeze(2).broadcast_to([128, 4, 128]), op=ALU.mult)
                DVE(["sq", "gains", "onb0"], ["sq"]).tensor_tensor(out=sq, in0=sq, in1=onb, op=ALU.mult)
                DVE(["sq", f"sgt{b}"], [f"yb{b}"]).tensor_tensor(out=yb[b], in0=sq, in1=sgt[b], op=ALU.mult)
                for j in range(4):
                    PE([f"yb{b}", "cmb"], [PSK[4 + b]]).transpose(psb[4 + b][:, j * 128:(j + 1) * 128], yb[b][:, j * 128:(j + 1) * 128], IDB)
                ACT([PSK[4 + b]], ["GT"]).activation(out=GT[:, :, i * 128:(i + 1) * 128], in_=psb[4 + b][:, 0:512].rearrange("p (m t) -> p m t", m=4), func=AF.Copy)

            fin_mm(0)
            for i in range(NT):
                if i + 1 < NT:
                    fin_mm(i + 1)
                fin_tile(i)
            S.barrier()

        def phase_H(l):
            w_in = dr["w_in"][l]
            A.top = HBASE
            w3 = A.bf16(2048)
            S.dma("pool", w3[0:64, :], dr["hy_w3"][l], "hw3", writes=["hw3"])
            S.dma("pool", w3[64:65, :], dr["hy_b3"][l].unsqueeze(0), "hw3", writes=["hw3"])
            cols = A.f32(4)
            LD(cols[0:64, 0:1], dr["hy_b1"][l].unsqueeze(1), "hcols")
            LD(cols[0:64, 1:2], dr["hy_b2"][l].unsqueeze(1), "hcols")
            LD(cols[0:64, 2:3], dr["hy_freq"][l, 0].unsqueeze(1), "hcols")
            LD(cols[0:64, 3:4], dr["hy_freq"][l, 1].unsqueeze(1), "hcols")
            H2 = A.bf16(T)
            DVE([], ["H2"]).memset(H2[64:96, :], 1.0)
            TOP1 = A.top
            zT = A.f32(T)
            LD(zT[0:33, :], dr["c_zT"], "zT")
            w1 = A.f32(64)
            LD(w1[0:33, :], dr["hy_w1"][l], "hw1")
            w2 = A.f32(64)
            LD(w2[0:64, :], dr["hy_w2"][l], "hw2")
            H1 = A.f32(T)
            arg = A.f32(T)
            for layer in range(2):
                src_, wt, kk, dst, bcol, fcol = (zT, w1, 33, H1, 0, 2) if layer == 0 else (H1, w2, 64, H2, 1, 3)
                dk = "H1" if layer == 0 else "H2"
                for tb in range(4):
                    PE(["zT", "hw1", "hw2", "H1"], [PSK[tb]]).matmul(ps[tb][0:64, :], wt[0:kk, :], src_[0:kk, tb * 512:(tb + 1) * 512], start=True, stop=True)
                    ACT([PSK[tb], "hcols"], ["arg"]).activation(out=arg[0:64, tb * 512:(tb + 1) * 512], in_=ps[tb][0:64, :], func=AF.Identity,
                                                               bias=cols[0:64, bcol:bcol + 1], scale=1.0)
                    ACT(["arg", "hcols"], ["arg"]).activation(out=arg[0:64, tb * 512:(tb + 1) * 512], in_=arg[0:64, tb * 512:(tb + 1) * 512], func=AF.Copy,
                                                             scale=cols[0:64, fcol:fcol + 1])
                a64, m64 = arg[0:64, :], dst[0:64, :]
                DVE(["arg"], [dk]).tensor_single_scalar(out=m64, in_=a64, scalar=math.pi, op=ALU.is_gt)
                DVE(["arg", dk], ["arg"]).scalar_tensor_tensor(out=a64, in0=m64, scalar=-2.0 * math.pi, in1=a64, op0=ALU.mult, op1=ALU.add)
                DVE(["arg"], [dk]).tensor_single_scalar(out=m64, in_=a64, scalar=-math.pi, op=ALU.is_lt)
                DVE(["arg", dk], ["arg"]).scalar_tensor_tensor(out=a64, in0=m64, scalar=2.0 * math.pi, in1=a64, op0=ALU.mult, op1=ALU.add)
                ACT(["arg"], [dk]).activation(out=m64, in_=a64, func=AF.Sin)
            S.barrier()
            A.top = TOP1
            fsd = A.bf16(NT * 2 * 2 * 512).rearrange("p (i a o c) -> p i a o c", i=NT, a=2, o=2)
            filt_f = A.f32(2048)
            absf_f = A.bf16(2048)
            filt = filt_f.rearrange("p (o s c) -> p o s c", o=2, s=2)
            absf = absf_f.rearrange("p (o s c) -> p o s c", o=2, s=2)
            win = [A.f32(512), A.f32(512)]
            rn = A.f32(1024).rearrange("p (o c) -> p o c", o=2)

            def h0_tile(i):
                b = i % 2
                LD(win[b], dr["c_win"][i * 128:(i + 1) * 128, :], f"win{b}")
                for cb in range(4):
                    PE(["H2", "hw3"], [PSK[cb]]).matmul(ps[cb][:, :], H2[0:65, i * 128:(i + 1) * 128], w3[0:65, cb * 512:(cb + 1) * 512], start=True, stop=True)
                    DVE([PSK[cb], f"win{b}"], ["filt"]).tensor_tensor(out=filt[:, cb // 2, cb % 2, :], in0=ps[cb][:, :], in1=win[b], op=ALU.mult)
                if i == 0:
                    DVE(["filt"], ["filt"]).memset(filt[0:1, :, 1, :], 0.0)
                ACT(["filt"], ["absf"]).activation(out=absf_f, in_=filt_f, func=AF.Abs)
                for o in range(2):
                    for s_ in range(2):
                        PE(["absf", "ones_b"], [PSK[4 + o]]).matmul(ps[4 + o][:, :], ones_b, absf[:, o, s_, :], start=(i == 0 and s_ == 0), stop=(i == NT - 1 and s_ == 1))
                DVE(["filt"], ["fsd"]).tensor_tensor(out=fsd[:, i, 0, :, :], in0=filt[:, :, 0, :], in1=filt[:, :, 1, :], op=ALU.add)
                POOL(["filt"], ["fsd"]).tensor_tensor(out=fsd[:, i, 1, :, :], in0=filt[:, :, 0, :], in1=filt[:, :, 1, :], op=ALU.subtract)

            for i in range(NT):
                h0_tile(i)
            for o in range(2):
                DVE([PSK[4 + o]], ["rn"]).tensor_scalar(out=rn[:, o, :], in0=ps[4 + o][:, :], scalar1=EPS, scalar2=None, op0=ALU.add)
            DVE(["rn"], ["rn"]).reciprocal(out=rn, in_=rn)
            S.barrier()
            ftb = [A.bf16(2 * NT * 128).rearrange("p (a i f) -> p a i f", a=2, i=NT) for _ in range(2)]
            spo = [A.f32(2048).rearrange("p (o a c) -> p o a c", o=2, a=2) for _ in range(2)]

            def h0_ld(j):
                b = j % 2
                LD(ftb[b][:, 0, :, :], dr["c_ft"][j, 0], f"ft{b}")
                LD(ftb[b][:, 1, :, :], dr["c_ft"][j, 1], f"ft{b}")

            def h0_spec(j):
                b = j % 2
                if j + 1 < 16:
                    h0_ld(j + 1)
                for o in range(2):
                    for a in range(2):
                        bank = (j % 2) * 4 + o * 2 + a
                        for i in range(NT):
                            PE([f"ft{b}", "fsd"], [PSK[bank]]).matmul(ps[bank][:, :], ftb[b][:, a, i, :], fsd[:, i, a, o, :], start=(i == 0), stop=(i == NT - 1))
                        DVE([PSK[bank], "rn"], [f"spo{b}"]).tensor_tensor(out=spo[b][:, o, a, :], in0=ps[bank][:, :], in1=rn[:, o, :], op=ALU.mult)
                for o in range(2):
                    ST(specD[o, j].rearrange("a p c -> p a c"), spo[b][:, o, :, :], f"spo{b}", ("specD", o, j))

            h0_ld(0)
            for j in range(16):
                h0_spec(j)
            S.barrier()
            A.top = HBASE
            vT = A.bf16(4 * T).rearrange("p (m t) -> p m t", m=4)
            vt = A.bf16(NT * 512).rearrange("p (i c) -> p i c", i=NT)
            zTt = A.bf16(4 * T).rearrange("p (m t) -> p m t", m=4)
            skipc = A.f32(8).rearrange("p (o m) -> p o m", o=2)
            for o in range(2):
                for m in range(4):
                    LD(skipc[:, o, m:m + 1], dr["hy_skip"][l, o, m * 128:(m + 1) * 128].unsqueeze(1), "skipc")
            H1TOP = A.top
            whb = A.bf16(8 * 1536).rearrange("p (k n) -> p k n", k=8)
            load_w(whb, w_in[:, C_HY:C_HY + 1536], "whb")
            cw = A.f32(36).rearrange("p (c k) -> p c k", c=12)
            for cc in range(12):
                for k in range(3):
                    LD(cw[:, cc, k:k + 1], dr["hy_conv"][l, k, cc * 128:(cc + 1) * 128].unsqueeze(1), "cw")
            upad = [A.f32(T + 2), A.f32(T + 2)]
            for b in range(2):
                DVE([], [f"upad{b}"]).memset(upad[b], 0.0)
            ctmp = [A.f32(T), A.f32(T)]
            xo = [A.bf16(T), A.bf16(T)]

            def h1_chunk(cc):
                b = cc % 2
                for tb in range(4):
                    bank = (cc % 2) * 4 + tb
                    for k in range(8):
                        PE(HN_ALL + ["whb"], [PSK[bank]]).matmul(ps[bank][:, :], whb[:, k, cc * 128:(cc + 1) * 128], hnT[:, k, tb * 512:(tb + 1) * 512], start=(k == 0), stop=(k == 7))
                    ACT([PSK[bank]], [f"upad{b}"]).activation(out=upad[b][:, 1 + tb * 512:1 + (tb + 1) * 512], in_=ps[bank][:, :], func=AF.Copy)
                u = upad[b]
                E = DVE if b == 0 else POOL
                ACT([f"upad{b}", "cw"], [f"ctmp{b}"]).activation(out=ctmp[b], in_=u[:, 1:T + 1], func=AF.Copy, scale=cw[:, cc, 1:2])
                DVE([f"upad{b}", "cw", f"ctmp{b}"], [f"ctmp{b}"]).scalar_tensor_tensor(out=ctmp[b], in0=u[:, 0:T], scalar=cw[:, cc, 0:1], in1=ctmp[b], op0=ALU.mult, op1=ALU.add)
                if cc < 4:
                    DVE([f"upad{b}", "cw", f"ctmp{b}"], ["vT"]).scalar_tensor_tensor(out=vT[:, cc, :], in0=u[:, 2:T + 2], scalar=cw[:, cc, 2:3], in1=ctmp[b], op0=ALU.mult, op1=ALU.add)
                else:
                    DVE([f"upad{b}", "cw", f"ctmp{b}"], [f"xo{b}"]).scalar_tensor_tensor(out=xo[b], in0=u[:, 2:T + 2], scalar=cw[:, cc, 2:3], in1=ctmp[b], op0=ALU.mult, op1=ALU.add)
                    ST(hyD[cc - 4], xo[b], f"xo{b}", ("hyD", cc - 4))

            for cc in range(12):
                h1_chunk(cc)
            S.barrier()
            A.top = H1TOP
            Y = A.bf16(32 * 512).rearrange("p (j c) -> p j c", j=32)
            ftb = [A.bf16(2 * NT * 128).rearrange("p (a i f) -> p a i f", a=2, i=NT) for _ in range(2)]
            spb = [A.f32(1024).rearrange("p (a c) -> p a c", a=2) for _ in range(2)]
            gbuf = A.bf16(32 * 512).rearrange("p (j t) -> p j t", j=32)
            tm = [A.f32(512) for _ in range(4)]
            xb = [A.bf16(512), A.bf16(512)]
            ytmp = [A.f32(512), A.f32(512)]

            def to_tokmajor(srcT, skey, dkey):
                for i in range(NT):
                    bank = 6 + (i % 2)
                    for m in range(4):
                        PE([skey, "cmb"], [PSK[bank]]).transpose(psb[bank][:, m * 128:(m + 1) * 128], srcT[:, m, i * 128:(i + 1) * 128], IDB)
                    ACT([PSK[bank]], [dkey]).activation(out=vt[:, i, :], in_=psb[bank][:, 0:512], func=AF.Copy)

            def longconv(o, srcT, skey, xbase, dstT, dkey):
                def fwd(j):
                    b = j % 2
                    LD(ftb[b][:, 0, :, :], dr["c_ft"][j, 0], f"ft{b}")
                    LD(ftb[b][:, 1, :, :], dr["c_ft"][j, 1], f"ft{b}")
                    LD(spb[b], specD[o, j].rearrange("a p c -> p a c"), f"spb{b}", reads=[("specD", o, j)])
                    for a in range(2):
                        bank = b * 2 + a
                        for i in range(NT):
                            PE([f"ft{b}", "vt"], [PSK[bank]]).matmul(ps[bank][:, :], ftb[b][:, a, i, :], vt[:, i, :], start=(i == 0), stop=(i == NT - 1))
                    pr, pi = ps[b * 2][:, :], ps[b * 2 + 1][:, :]
                    fr, fi = spb[b][:, 0, :], spb[b][:, 1, :]
                    kr, ki = PSK[b * 2], PSK[b * 2 + 1]
                    DVE([kr, f"spb{b}"], ["tm0"]).tensor_tensor(out=tm[0], in0=pr, in1=fr, op=ALU.mult)
                    DVE([ki, f"spb{b}"], ["tm1"]).tensor_tensor(out=tm[1], in0=pi, in1=fi, op=ALU.mult)
                    DVE([kr, f"spb{b}"], ["tm2"]).tensor_tensor(out=tm[2], in0=pr, in1=fi, op=ALU.mult)
                    DVE([ki, f"spb{b}"], ["tm3"]).tensor_tensor(out=tm[3], in0=pi, in1=fr, op=ALU.mult)
                    POOL(["tm0", "tm1"], ["Y"]).tensor_tensor(out=Y[:, j, :], in0=tm[0], in1=tm[1], op=ALU.subtract)
                    POOL(["tm2", "tm3"], ["Y"]).tensor_tensor(out=Y[:, 16 + j, :], in0=tm[2], in1=tm[3], op=ALU.add)

                for j in range(16):
                    fwd(j)

                def inv(tb, it0):
                    for q4 in range(4):
                        LD(gbuf[:, q4 * 8:(q4 + 1) * 8, :], dr["c_g"][tb, :, q4 * 8:(q4 + 1) * 8, :], ("gbuf", q4))
                    for cc in range(4):
                        it = it0 + cc
                        bank = 4 + (it % 2)
                        b = it % 2
                        LD(xb[b], hyD[xbase + cc][:, tb * 512:(tb + 1) * 512], f"xb{b}", reads=[("hyD", xbase + cc)])
                        for jj in range(32):
                            PE(["Y", ("gbuf", jj // 8)], [PSK[bank]]).matmul(ps[bank][:, :], Y[:, jj, cc * 128:(cc + 1) * 128], gbuf[:, jj, :], start=(jj == 0), stop=(jj == 31))
                        DVE([skey, "skipc", PSK[bank]], [f"ytmp{b}"]).scalar_tensor_tensor(out=ytmp[b], in0=srcT[:, cc, tb * 512:(tb + 1) * 512], scalar=skipc[:, o, cc:cc + 1],
                                                                                         in1=ps[bank][:, :], op0=ALU.mult, op1=ALU.add)
                        DVE([f"ytmp{b}", f"xb{b}"], [dkey]).tensor_tensor(out=dstT[:, cc, tb * 512:(tb + 1) * 512], in0=ytmp[b], in1=xb[b], op=ALU.mult)

                for tb in range(4):
                    inv(tb, tb * 4)

            to_tokmajor(vT, "vT", "vt")
            longconv(0, vT, "vT", 0, zTt, "zTt")
            to_tokmajor(zTt, "zTt", "vt")
            longconv(1, zTt, "zTt", 4, HT, "HT")
            S.barrier()

        def resid_mm(i, src_h, hb, hkey, pbanks, lhs_fn, rhs_fn, nk, rkeys):
            LD(hb, src_h[i * 128:(i + 1) * 128, :], hkey, reads=[("hD", i)])
            for half in range(2):
                bank = pbanks[half]
                for k in range(nk):
                    PE(rkeys, [PSK[bank]]).matmul(ps[bank][:, :], lhs_fn(k), rhs_fn(k, half), start=(k == 0), stop=(k == nk - 1))

        def resid_add(i, hb, hkey, pbanks, dst_h):
            for half in range(2):
                bank = pbanks[half]
                DVE([PSK[bank], hkey], [hkey]).tensor_tensor(out=hb[:, half * 512:(half + 1) * 512], in0=ps[bank][:, :], in1=hb[:, half * 512:(half + 1) * 512], op=ALU.add)
            ST(dst_h[i * 128:(i + 1) * 128, :], hb, hkey, ("hD", i))

        def resid_loop(src_h, hb, lhs_of, rhs_fn, rkeys, dst_h, n, nbufs=3):
            def mm(i):
                b = i % nbufs
                resid_mm(i, src_h, hb[b], f"hb{b}", (2 * (i % 2), 2 * (i % 2) + 1), lhs_of(i), rhs_fn, 8, rkeys)
            mm(0)
            for i in range(NT):
                if i + 1 < NT:
                    mm(i + 1)
                b = i % nbufs
                resid_add(i, hb[b], f"hb{b}", (2 * (i % 2), 2 * (i % 2) + 1), dst_h)
                norm_A(n, hb[b], f"hb{b}", i)
                if i >= 1:
                    pb_ = (i - 1) % nbufs
                    norm_B(n, hb[pb_], f"hb{pb_}", i - 1, hnT, ("hnT", i - 1), 6 + ((i - 1) % 2))
            pb_ = (NT - 1) % nbufs
            norm_B(n, hb[pb_], f"hb{pb_}", NT - 1, hnT, ("hnT", NT - 1), 6 + ((NT - 1) % 2))

        def phase_M(l):
            w_in = dr["w_in"][l]
            A.top = BASE
            mixT = A.bf16(8 * T).rearrange("p (k t) -> p k t", k=8)
            MTOP = A.top
            wbr = A.bf16(3 * 4 * D).rearrange("p (b m n) -> p b m n", b=3, m=4)
            for m in range(4):
                for g in range(2):
                    r0 = (g * 4 + m) * 64
                    S.dma("pool", wbr[g * 64:(g + 1) * 64, 0, m, :], dr["w_br_attn"][l, r0:r0 + 64, :], "wbr", writes=["wbr"])
            load_w(wbr[:, 1, :, :], dr["w_br_hyena"][l], "wbr")
            load_w(wbr[:, 2, :, :], dr["w_br_gla"][l], "wbr")
            WG_TOP = ARENA_W - 12288
            wgall = arena_t[:, WG_TOP:ARENA_W].bitcast(BF16).rearrange("p (k n) -> p k n", k=8)
            if "A" not in phases:
                for k in range(8):
                    for hf in range(2):
                        S.dma("pool", wgall[:, k, hf * 1536:(hf + 1) * 1536], w_in[k * 128:(k + 1) * 128, C_GATE + hf * 1536:C_GATE + (hf + 1) * 1536], "wgall", writes=["wgall"])
            sig = [A.f32(512), A.f32(512)]
            tt = [A.f32(512), A.f32(512)]
            acc = A.f32(512)
            assert A.top <= WG_TOP, A.top
            brT = [OT, HT, GT]
            brK = ["OT", "HT", "GT"]
            stp = Ctx()
            stp.n = 0

            def m_iter(oc, tb):
                tk = slice(tb * 512, (tb + 1) * 512)
                wk = "wgall"
                for b in range(3):
                    r = stp.n % 4
                    stp.n += 1
                    by, bg = 2 * r, 2 * r + 1
                    for m in range(4):
                        PE(["wbr", brK[b]], [PSK[by]]).matmul(ps[by][:, :], wbr[:, b, m, oc * 128:(oc + 1) * 128], brT[b][:, m, tk], start=(m == 0), stop=(m == 3))
                    for k in range(8):
                        PE(HN_ALL + [wk], [PSK[bg]]).matmul(ps[bg][:, :], wgall[:, k, b * D + oc * 128:b * D + (oc + 1) * 128], hnT[:, k, tk], start=(k == 0), stop=(k == 7))
                    sb = sig[r % 2]
                    ACT([PSK[bg]], [f"sig{r % 2}"]).activation(out=sb, in_=ps[bg][:, :], func=AF.Sigmoid)
                    if b == 0:
                        DVE([PSK[by], f"sig{r % 2}"], ["acc"]).tensor_tensor(out=acc, in0=ps[by][:, :], in1=sb, op=ALU.mult)
                    else:
                        DVE([PSK[by], f"sig{r % 2}"], [f"tt{r % 2}"]).tensor_tensor(out=tt[r % 2], in0=ps[by][:, :], in1=sb, op=ALU.mult)
                        if b == 1:
                            POOL(["acc", f"tt{r % 2}"], ["acc"]).tensor_tensor(out=acc, in0=acc, in1=tt[r % 2], op=ALU.add)
                        else:
                            POOL(["acc", f"tt{r % 2}"], ["mixT"]).tensor_tensor(out=mixT[:, oc, tk], in0=acc, in1=tt[r % 2], op=ALU.add)

            for oc in range(8):
                for tb in range(4):
                    m_iter(oc, tb)
            S.barrier()
            A.top = MTOP
            wout = A.bf16(8 * D).rearrange("p (k n) -> p k n", k=8)
            load_w(wout, dr["w_out"][l], "wout")
            n = norm_setup(dr["ln_x"][l])
            hb = [A.f32(D), A.f32(D), A.f32(D)]
            resid_loop((dr["x"] if l == 0 else hbuf), hb, lambda i: (lambda k: mixT[:, k, i * 128:(i + 1) * 128]),
                       lambda k, half: wout[:, k, half * 512:(half + 1) * 512], ["mixT", "wout"], hbuf, n)
            S.barrier()

        def phase_X(l):
            W1TOP = ARENA_W - 16384
            A.top = HNBASE
            kxT = A.bf16(8 * MEM).rearrange("p (k t) -> p k t", k=8)
            vx = A.bf16(2 * D).rearrange("p (i c) -> p i c", i=2)
            gains = A.f32(D)
            gtmp = A.f32(256)
            sq = A.f32(D)
            qsb = [A.f32(D), A.f32(D)]
            nrm = [A.bf16(D), A.bf16(D)]
            ss4 = [A.f32(4), A.f32(4)]
            XP = A.top
            wk = A.bf16(8 * D).rearrange("p (k n) -> p k n", k=8)
            wv = A.bf16(8 * D).rearrange("p (k n) -> p k n", k=8)
            load_w(wk, dr["x_wk"][l], "wk")
            load_w(wv, dr["x_wv"][l], "wv")
            mnT = A.bf16(8 * MEM).rearrange("p (k t) -> p k t", k=8)
            n = norm_setup(dr["ln_mem"][l])
            mb = [A.f32(D), A.f32(D)]
            assert A.top <= W1TOP
            bc_load(gtmp, dr["x_knorm"][l], "gtmp")
            DVE(["gtmp"], ["gains"]).tensor_copy(out=gains.rearrange("p (h d) -> p h d", h=4), in_=gtmp.unsqueeze(1).broadcast_to([128, 4, 256]))
            for mt in range(2):
                LD(mb[mt], dr["mem"][mt * 128:(mt + 1) * 128, :], f"mb{mt}")
                norm_tile(n, mb[mt], f"mb{mt}", mt * 128, mnT, "mnT", 6 + mt)
            for mt in range(2):
                tok = slice(mt * 128, (mt + 1) * 128)
                for half in range(2):
                    for k in range(8):
                        PE(["mnT", "wk"], [PSK[half]]).matmul(ps[half][:, :], mnT[:, k, tok], wk[:, k, half * 512:(half + 1) * 512], start=(k == 0), stop=(k == 7))
                    ACT([PSK[half]], [f"qsb{mt}"]).activation(out=qsb[mt][:, half * 512:(half + 1) * 512], in_=ps[half][:, :], func=AF.Copy)
                    for k in range(8):
                        PE(["mnT", "wv"], [PSK[2 + half]]).matmul(ps[2 + half][:, :], mnT[:, k, tok], wv[:, k, half * 512:(half + 1) * 512], start=(k == 0), stop=(k == 7))
                    ACT([PSK[2 + half]], ["vx"]).activation(out=vx[:, mt, half * 512:(half + 1) * 512], in_=ps[2 + half][:, :], func=AF.Copy)
                headnorm(qsb[mt], 4, 256, gains, nrm[mt], f"qsb{mt}", f"nrm{mt}", sq, ss4[mt], f"ss4{mt}")
                for j in range(8):
                    PE([f"nrm{mt}", "cmb"], [PSK[4 + mt]]).transpose(psb[4 + mt][:, j * 128:(j + 1) * 128], nrm[mt][:, j * 128:(j + 1) * 128], IDB)
                ACT([PSK[4 + mt]], ["kxT"]).activation(out=kxT[:, :, tok], in_=psb[4 + mt][:, 0:1024].rearrange("p (k t) -> p k t", k=8), func=AF.Copy)
            S.barrier()
            A.top = XP
            qxT = A.bf16(8 * T).rearrange("p (k t) -> p k t", k=8)
            wq = A.bf16(8 * D).rearrange("p (k n) -> p k n", k=8)
            load_w(wq, dr["x_wq"][l], "wq")
            w1 = arena_t[:, W1TOP:ARENA_W].bitcast(BF16).rearrange("p (k n) -> p k n", k=8)
            for q4 in range(4):
                load_w(w1[:, :, q4 * D:(q4 + 1) * D], dr["mlp_w1"][l][:, q4 * D:(q4 + 1) * D], "w1")
            bc_load(gtmp, dr["x_qnorm"][l], "gtmp")
            DVE(["gtmp"], ["gains"]).tensor_scalar(out=gains.rearrange("p (h d) -> p h d", h=4), in0=gtmp.unsqueeze(1).broadcast_to([128, 4, 256]),
                                                   scalar1=1.0 / 16, scalar2=None, op0=ALU.mult)

            def xq_mm(i):
                b = i % 2
                tok = slice(i * 128, (i + 1) * 128)
                for half in range(2):
                    bank = 2 * b + half
                    for k in range(8):
                        PE([("hnT", i), "wq"], [PSK[bank]]).matmul(ps[bank][:, :], hnT[:, k, tok], wq[:, k, half * 512:(half + 1) * 512], start=(k == 0), stop=(k == 7))

            sqB = A.f32(D)
            v3x = lambda a: a.rearrange("p (h d) -> p h d", h=4)

            def xq_stA(i):
                b = i % 2
                for half in range(2):
                    bank = 2 * b + half
                    ACT([PSK[bank]], [f"qsb{b}"]).activation(out=qsb[b][:, half * 512:(half + 1) * 512], in_=ps[bank][:, :], func=AF.Copy)
                DVE([f"qsb{b}"], ["hsq"]).tensor_tensor(out=sq, in0=qsb[b], in1=qsb[b], op=ALU.mult)
                DVE(["hsq"], [f"ss4{b}"]).tensor_reduce(out=ss4[b], in_=v3x(sq), axis=AX.X, op=ALU.add)
                rstd_from_ss(ss4[b], 256.0, f"ss4{b}")

            def xq_stB(i):
                b = i % 2
                tok = slice(i * 128, (i + 1) * 128)
                DVE([f"qsb{b}", f"ss4{b}"], ["sqB"]).tensor_tensor(out=v3x(sqB), in0=v3x(qsb[b]), in1=ss4[b].unsqueeze(2).broadcast_to([128, 4, 256]), op=ALU.mult)
                DVE(["sqB", "gains"], [f"nrm{b}"]).tensor_tensor(out=nrm[b], in0=sqB, in1=gains, op=ALU.mult)
                for j in range(8):
                    PE([f"nrm{b}", "cmb"], [PSK[4 + b]]).transpose(psb[4 + b][:, j * 128:(j + 1) * 128], nrm[b][:, j * 128:(j + 1) * 128], IDB)
                ACT([PSK[4 + b]], [("qx", c, i // 4) for c in range(8)]).activation(out=qxT[:, :, tok], in_=psb[4 + b][:, 0:1024].rearrange("p (k t) -> p k t", k=8), func=AF.Copy)

            xq_mm(0)
            xq_stA(0)
            xq_mm(1)
            for i in range(NT):
                if i + 1 < NT:
                    xq_stA(i + 1)
                if i + 2 < NT:
                    xq_mm(i + 2)
                xq_stB(i)
            S.barrier(keep=["w1"])
            A.top = XP + 8192
            wo = A.bf16(8 * D).rearrange("p (k n) -> p k n", k=8)
            load_w(wo, dr["x_wo"][l], "wo")
            pT = [A.bf16(512) for _ in range(4)]
            rden = [A.f32(512), A.f32(512)]
            lnd = [A.f32(512), A.f32(512)]

            def xc_scores(hd, qb, it):
                qs = slice(qb * 512, (qb + 1) * 512)
                b = it % 2
                for mt in range(2):
                    bank = 4 * b + mt
                    for half in range(2):
                        PE(["kxT", ("qx", hd * 2 + half, qb)], [PSK[bank]]).matmul(ps[bank][:, :], kxT[:, hd * 2 + half, mt * 128:(mt + 1) * 128], qxT[:, hd * 2 + half, qs],
                                                                                 start=(half == 0), stop=(half == 1))
                    ACT([PSK[bank]], [f"pT{2 * b + mt}"]).activation(out=pT[2 * b + mt], in_=ps[bank][:, :], func=AF.Exp, bias=-8.0, scale=1.0)

            def xc_out(hd, qb, it):
                qs = slice(qb * 512, (qb + 1) * 512)
                b = it % 2
                bden = 4 * b + 2
                for mt in range(2):
                    PE([f"pT{2 * b + mt}", "ones_b"], [PSK[bden]]).matmul(ps[bden][:, :], ones_b, pT[2 * b + mt], start=(mt == 0), stop=(mt == 1))
                ACT([PSK[bden]], [f"lnd{b}"]).activation(out=lnd[b], in_=ps[bden][:, :], func=AF.Ln)
                ACT([f"lnd{b}"], [f"rden{b}"]).activation(out=rden[b], in_=lnd[b], func=AF.Exp, scale=-1.0)
                for half in range(2):
                    bo = 4 * b + 3 if half == 0 else 4 * b
                    for mt in range(2):
                        PE([f"pT{2 * b}", f"pT{2 * b + 1}", "vx"], [PSK[bo]]).matmul(ps[bo][:, :], vx[:, mt, hd * 256 + half * 128:hd * 256 + (half + 1) * 128], pT[2 * b + mt],
                                                                                  start=(mt == 0), stop=(mt == 1))
                    DVE([PSK[bo], f"rden{b}"], [("qx", hd * 2 + half, qb)]).tensor_tensor(out=qxT[:, hd * 2 + half, qs], in0=ps[bo][:, :], in1=rden[b], op=ALU.mult)

            its = [(hd, qb) for hd in range(4) for qb in range(4)]
            xc_scores(its[0][0], its[0][1], 0)
            for it, (hd, qb) in enumerate(its):
                if it + 1 < len(its):
                    xc_scores(its[it + 1][0], its[it + 1][1], it + 1)
                xc_out(hd, qb, it)
            n = norm_setup(dr["ln_mlp"][l])
            assert A.top <= W1TOP, A.top
            hb = [qsb[0], qsb[1], sq]
            QX_ALL = [("qx", c, qb) for c in range(8) for qb in range(4)]
            S.barrier(keep=["w1"])
            resid_loop(hbuf, hb, lambda i: (lambda k: qxT[:, k, i * 128:(i + 1) * 128]),
                       lambda k, half: wo[:, k, half * 512:(half + 1) * 512], ["wo"] + QX_ALL, hbuf, n)
            S.barrier(keep=["w1"])

        def phase_F(l, last):
            A.top = HNBASE
            W1TOP = ARENA_W - 16384
            w1 = arena_t[:, W1TOP:ARENA_W].bitcast(BF16).rearrange("p (k n) -> p k n", k=8)
            if "X" not in phases:
                for q4 in range(4):
                    load_w(w1[:, :, q4 * D:(q4 + 1) * D], dr["mlp_w1"][l][:, q4 * D:(q4 + 1) * D], "w1")
            w2 = A.bf16(32 * D).rearrange("p (j n) -> p j n", j=32)
            load_w(w2, dr["mlp_w2"][l], "w2")
            hid = A.bf16(16 * 256).rearrange("p (j t) -> p j t", j=16)
            rl = [A.bf16(256), A.bf16(256)]
            hb = [A.f32(D), A.f32(D)]
            n = None if last else norm_setup(dr["ln_mix"][l + 1])
            dst = out if last else hbuf
            stp = Ctx()
            stp.n = 0

            def f_block(tb):
                tk = slice(tb * 256, (tb + 1) * 256)
                for fh in range(2):
                    for j in range(16):
                        jj = fh * 16 + j
                        bank = 4 + (stp.n % 4)
                        r = stp.n % 2
                        stp.n += 1
                        for k in range(8):
                            PE([("hnT", 2 * tb), ("hnT", 2 * tb + 1), "w1"], [PSK[bank]]).matmul(ps[bank][:, 0:256], w1[:, k, jj * 128:(jj + 1) * 128], hnT[:, k, tk], start=(k == 0), stop=(k == 7))
                        ACT([PSK[bank]], [f"rl{r}"]).activation(out=rl[r], in_=ps[bank][:, 0:256], func=AF.Relu)
                        POOL([f"rl{r}"], [("hid", j)]).tensor_tensor(out=hid[:, j, :], in0=rl[r], in1=rl[r], op=ALU.mult)
                    for tt_ in range(2):
                        for half in range(2):
                            bank = tt_ * 2 + half
                            for j in range(16):
                                jj = fh * 16 + j
                                PE([("hid", j), "w2"], [PSK[bank]]).matmul(ps[bank][:, :], hid[:, j, tt_ * 128:(tt_ + 1) * 128], w2[:, jj, half * 512:(half + 1) * 512],
                                                                           start=(fh == 0 and j == 0), stop=(fh == 1 and j == 15))
                for tt_ in range(2):
                    i = tb * 2 + tt_
                    b = i % 2
                    LD(hb[b], hbuf[i * 128:(i + 1) * 128, :], f"hb{b}", reads=[("hD", i)])
                    for half in range(2):
                        bank = tt_ * 2 + half
                        DVE([PSK[bank], f"hb{b}"], [f"hb{b}"]).tensor_tensor(out=hb[b][:, half * 512:(half + 1) * 512], in0=ps[bank][:, :], in1=hb[b][:, half * 512:(half + 1) * 512], op=ALU.add)
                    ST(dst[i * 128:(i + 1) * 128, :], hb[b], f"hb{b}", ("hD", i))
                    if not last:
                        norm_tile(n, hb[b], f"hb{b}", i * 128, hnT, ("hnT", i), 4 + b)

            for tb in range(8):
                f_block(tb)
            S.barrier()

        phase_N1()
        for l in range(nlayers):
            if "H" in phases:
                phase_H(l)
            if "G" in phases:
                phase_G(l)
            if "A" in phases:
                phase_A(l)
            if "dbgOT" in dbg_aps:
                ST(dbg_aps["dbgOT"].rearrange("(m p) t -> p m t", p=128), OT, "OT", "dbgOT")
            if "dbgG" in dbg_aps:
                ST(dbg_aps["dbgG"].rearrange("(m p) t -> p m t", p=128), GT, "GT", "dbgG")
            if "dbgH" in dbg_aps:
                ST(dbg_aps["dbgH"].rearrange("(m p) t -> p m t", p=128), HT, "HT", "dbgH")
            if "dbg_hnT" in dbg_aps:
                S.dma("sp", dbg_aps["dbg_hnT"].rearrange("(k p) t -> p k t", p=128), hnT, "dbg1", reads=HN_ALL, writes=["dbg1"])
            S.barrier(keep=["wgall"])
            if "M" in phases:
                phase_M(l)
            if "dbg_h1" in dbg_aps and l == 0:
                S.dma("sp", dbg_aps["dbg_h1"], hbuf, "dbgh1", reads=[], writes=["dbgh1"])
                S.barrier()
            if "X" in phases:
                phase_X(l)
            if "dbg_h2" in dbg_aps and l == 0:
                S.dma("sp", dbg_aps["dbg_h2"], hbuf, "dbgh2", reads=[], writes=["dbgh2"])
                S.barrier(keep=["w1"])
            if "F" in phases:
                phase_F(l, l == nlayers - 1)
        S.barrier()
        S.emit(st)
    return nc


def make_in_maps(inputs):
    hc = host_consts()
    maps = []
    for c in range(8):
        m = {"x": np.ascontiguousarray(inputs["x"][c], dtype=np.float32),
             "mem": np.ascontiguousarray(inputs["mem"][c], dtype=np.float32)}
        for name, _ in WEIGHT_SPECS:
            m[name] = np.ascontiguousarray(inputs[name], dtype=np.float32)
        m.update(hc)
        maps.append(m)
    return maps


def kernel(**inputs):
    inputs = {k: np.asarray(v) for k, v in inputs.items()}
    nc = build()
    res = run_bass_kernel_spmd(nc, make_in_maps(inputs), core_ids=list(range(8)))
    return np.stack([np.asarray(r["out"], dtype=np.float32) for r in res.results], axis=0)
```

```python
import math
from contextlib import ExitStack

import numpy as np
import ml_dtypes
import concourse.bass as bass
import concourse.mybir as mybir
from concourse.bass_utils import run_bass_kernel_spmd

F32 = mybir.dt.float32
BF16 = mybir.dt.bfloat16
AF = mybir.ActivationFunctionType
ALU = mybir.AluOpType
AX = mybir.AxisListType

ENGS = ("pe", "act", "dve", "pool", "sp")

D = 1024
T = 2048
NT = 16
DEPTH = 2
MEM = 256
N_IN = 6944
EPS = 1e-6
C_HY = 768
C_GQ = 2304
C_GATE = 3872
ARENA_W = 53000


class Sched:
    def __init__(self, nc):
        self.nc = nc
        self.ops = {e: [] for e in ENGS}
        self.cnt = {e: 0 for e in ENGS}
        self.seen = {e: {} for e in ENGS}
        self.lastw = {}
        self.readers = {}
        self.dma_sems = {}
        self.dma_names = {}
        self.sem_objs = {}

    def _deps(self, reads, writes):
        ev = []
        for k in reads:
            if k in self.lastw:
                ev.append(self.lastw[k] + (True,))
        for k in writes:
            if k in self.lastw:
                ev.append(self.lastw[k] + (False,))
            ev.extend(r + (False,) for r in self.readers.get(k, ()))
        return ev

    def _commit(self, event, reads, writes):
        for k in reads:
            self.readers.setdefault(k, []).append(event)
        for k in writes:
            self.lastw[k] = event
            self.readers[k] = []

    def _waits_for(self, eng, events):
        need = {}
        for evt in events:
            s, v = evt[0], evt[1]
            raw = evt[2] if len(evt) > 2 else True
            if s == eng and (eng == "pe" or not raw):
                continue
            if v > need.get(s, 0):
                need[s] = v
        out = []
        seen = self.seen[eng]
        for s, v in need.items():
            if seen.get(s, 0) >= v:
                continue
            seen[s] = v
            out.append((s, v))
        return out

    def op(self, eng, fn, reads=(), writes=()):
        events = self._deps(reads, writes)
        waits = self._waits_for(eng, events)
        self.cnt[eng] += 1
        event = (eng, self.cnt[eng])
        self.ops[eng].append((waits, fn, (eng, 1)))
        self._commit(event, reads, writes)
        return event

    def rec(self, eng, reads=(), writes=()):
        return _Rec(self, eng, list(reads), list(writes))

    def dma(self, q, out, in_, sem, reads=(), writes=(), **kw):
        events = self._deps(reads, writes)
        waits = self._waits_for(q, events)
        semname = self.dma_names.setdefault(sem, "dq%d" % len(self.dma_names))
        self.dma_sems[semname] = self.dma_sems.get(semname, 0) + 16
        event = (semname, self.dma_sems[semname])
        fn = lambda e, out=out, in_=in_, kw=kw: e.dma_start(out=out, in_=in_, **kw)
        self.ops[q].append((waits, fn, (semname, 16)))
        self._commit(event, reads, writes)
        return event

    def barrier(self, keep=()):
        skip = {self.lastw[k][0] for k in keep if k in self.lastw}
        events = [(e, self.cnt[e]) for e in ENGS if self.cnt[e] > 0]
        events += [(s, v) for s, v in self.dma_sems.items() if s not in skip]
        for e in ENGS:
            waits = self._waits_for(e, events)
            if waits:
                self.ops[e].append((waits, None, None))
        kept = {k: self.lastw[k] for k in keep if k in self.lastw}
        self.lastw = kept
        self.readers = {}

    def emit(self, stack):
        nc = self.nc
        for n in list(ENGS) + list(self.dma_sems.keys()):
            self.sem_objs[n] = stack.enter_context(nc.semaphore(n))
        block = stack.enter_context(nc.Block())
        sems = self.sem_objs

        def runner(engname):
            def body(eng):
                for waits, fn, inc in self.ops[engname]:
                    for s, v in waits:
                        eng.wait_ge(sems[s], v)
                    if fn is not None:
                        fn(eng).then_inc(sems[inc[0]], inc[1])
            return body

        block.tensor(runner("pe"))
        block.scalar(runner("act"))
        block.vector(runner("dve"))
        block.gpsimd(runner("pool"))
        block.sync(runner("sp"))


class _Rec:
    def __init__(self, S, eng, reads, writes):
        self.S, self.eng, self.reads, self.writes = S, eng, reads, writes

    def __getattr__(self, name):
        def f(*a, **k):
            return self.S.op(self.eng, lambda e: getattr(e, name)(*a, **k), self.reads, self.writes)
        return f


class Arena:
    def __init__(self, ap):
        self.ap = ap
        self.top = 0

    def f32(self, n):
        a = self.ap[:, self.top:self.top + n]
        self.top += n
        assert self.top <= ARENA_W, self.top
        return a

    def bf16(self, n):
        w = (n + 1) // 2
        a = self.ap[:, self.top:self.top + w].bitcast(BF16)
        self.top += w
        assert self.top <= ARENA_W, self.top
        return a


_CONSTS = None


def host_consts():
    global _CONSTS
    if _CONSTS is not None:
        return _CONSTS
    c = {}
    t = np.arange(T)
    inv = 10000.0 ** (-np.arange(16, dtype=np.float64) / 16)
    pos = np.stack([t // 64, t % 64], axis=1).astype(np.float64)
    ang = pos[:, :, None] * inv
    cs = np.concatenate([np.cos(ang).reshape(T, 32), np.sin(ang).reshape(T, 32)], axis=1)
    c["c_rope"] = np.ascontiguousarray(cs.reshape(NT, 128, 64).transpose(1, 0, 2)).astype(np.float32)
    tf = np.arange(T, dtype=np.float32)
    t_norm = tf / np.float32(T - 1)
    w = np.float32(2.0 * math.pi) * tf / np.float32(T)
    f = np.linspace(1e-4, 15, 16, dtype=np.float32)
    fw = w[:, None] * f
    z = np.concatenate([t_norm[:, None], np.cos(fw), -np.sin(fw)], axis=-1).astype(np.float32)
    c["c_zT"] = np.ascontiguousarray(z.T)
    deltas = np.abs(np.linspace(math.log(1e-2) / 0.3, math.log(1e-2) / 1.5, 512, dtype=np.float32))
    c["c_win"] = np.exp(-t_norm[:, None] * deltas).astype(np.float32)
    fidx = np.arange(2048, dtype=np.int64)
    ph = ((2 * fidx[:, None] + 1) * t[None, :].astype(np.int64)) % 8192
    angle = ph.astype(np.float64) * (2.0 * math.pi / 8192.0)
    Cm = np.cos(angle)
    Sm = np.sin(angle)
    ft = np.stack([Cm, Sm], axis=0).reshape(2, 16, 128, NT, 128)
    ft = ft.transpose(1, 0, 4, 3, 2)
    c["c_ft"] = np.ascontiguousarray(ft).astype(ml_dtypes.bfloat16)
    g = np.stack([Cm, Sm], axis=0).reshape(2, 16, 128, 4, 512) * (2.0 / 4096.0)
    g = g.transpose(3, 2, 0, 1, 4).reshape(4, 128, 32, 512)
    c["c_g"] = np.ascontiguousarray(g).astype(ml_dtypes.bfloat16)
    s_i = np.arange(128)[:, None]
    t_i = np.arange(128)[None, :]
    le = (s_i <= t_i).astype(np.float32)
    ge = (s_i >= t_i).astype(np.float32)
    gt = (s_i > t_i).astype(np.float32)
    lt = (s_i < t_i).astype(np.float32)
    ident = np.eye(128, dtype=np.float32)
    c["c_m"] = np.ascontiguousarray(np.stack([le, ge, -gt, -lt, gt, ident], axis=1)).astype(np.float32)
    _CONSTS = c
    return c


WEIGHT_SPECS = [
    ("ln_mix", (DEPTH, D)), ("w_in", (DEPTH, D, N_IN)), ("attn_qnorm", (DEPTH, 64)), ("attn_knorm", (DEPTH, 64)),
    ("hy_conv", (DEPTH, 3, 1536)), ("hy_w1", (DEPTH, 33, 64)), ("hy_b1", (DEPTH, 64)), ("hy_w2", (DEPTH, 64, 64)),
    ("hy_b2", (DEPTH, 64)), ("hy_w3", (DEPTH, 64, 2048)), ("hy_b3", (DEPTH, 2048)), ("hy_freq", (DEPTH, 2, 64)),
    ("hy_skip", (DEPTH, 2, 512)), ("gla_w_lr", (DEPTH, 2, 16, 256)), ("gla_b_lr", (DEPTH, 2, 256)),
    ("gla_onorm", (DEPTH, 128)), ("w_br_attn", (DEPTH, 512, D)), ("w_br_hyena", (DEPTH, 512, D)),
    ("w_br_gla", (DEPTH, 512, D)), ("w_out", (DEPTH, D, D)), ("ln_x", (DEPTH, D)), ("ln_mem", (DEPTH, D)),
    ("x_wq", (DEPTH, D, D)), ("x_wk", (DEPTH, D, D)), ("x_wv", (DEPTH, D, D)), ("x_wo", (DEPTH, D, D)),
    ("x_qnorm", (DEPTH, 256)), ("x_knorm", (DEPTH, 256)), ("ln_mlp", (DEPTH, D)),
    ("mlp_w1", (DEPTH, D, 4 * D)), ("mlp_w2", (DEPTH, 4 * D, D)),
]


def build(phases=("H", "A", "G", "M", "X", "F"), nlayers=DEPTH, dbg=()):
    nc = bass.Bass("TRN2", target_bir_lowering=False)
    dr = {}
    dr["x"] = nc.dram_tensor("x", [T, D], F32, kind="ExternalInput").ap()
    dr["mem"] = nc.dram_tensor("mem", [MEM, D], F32, kind="ExternalInput").ap()
    for name, shp in WEIGHT_SPECS:
        dr[name] = nc.dram_tensor(name, list(shp), F32, kind="ExternalInput").ap()
    hc = host_consts()
    for name, arr in hc.items():
        dt = BF16 if arr.dtype == ml_dtypes.bfloat16 else F32
        dr[name] = nc.dram_tensor(name, list(arr.shape), dt, kind="ExternalInput").ap()
    out = nc.dram_tensor("out", [T, D], F32, kind="ExternalOutput").ap()
    dbg_aps = {}
    for name, shp, dt in dbg:
        dbg_aps[name] = nc.dram_tensor(name, list(shp), dt, kind="ExternalOutput").ap()
    hbuf = nc.dram_tensor("hbuf", [T, D], F32, kind="Internal").ap()
    specD = nc.dram_tensor("specD", [2, 16, 2, 128, 512], F32, kind="Internal").ap()
    hyD = nc.dram_tensor("hyD", [8, 128, T], BF16, kind="Internal").ap()

    with ExitStack() as st:
        arena_t = st.enter_context(nc.sbuf_tensor("arena", [128, ARENA_W], F32))
        ps = [st.enter_context(nc.psum_tensor(f"ps{i}", [128, 512], F32)) for i in range(8)]
        psb = [p[:, :].bitcast(BF16) for p in ps]
        S = Sched(nc)
        A = Arena(arena_t)
        PSK = [f"ps{i}" for i in range(8)]
        PE = lambda r=(), w=(): S.rec("pe", r, w)
        ACT = lambda r=(), w=(): S.rec("act", r, w)
        DVE = lambda r=(), w=(): S.rec("dve", r, w)
        POOL = lambda r=(), w=(): S.rec("pool", r, w)

        def LD(dst, src, key, q="sp", reads=()):
            S.dma(q, dst, src, str(key), reads=reads, writes=[key])

        def ST(dst, src, key, dkey, q="sp"):
            S.dma(q, dst, src, "st_" + str(key), reads=[key], writes=[dkey])

        cm = A.f32(6 * 128).rearrange("p (a b) -> p a b", a=6)
        LD(cm, dr["c_m"], "cm")
        M_LE, M_GE, M_NGT, M_NLT, M_GT, IDF = (cm[:, i, :] for i in range(6))
        cmb = A.bf16(3 * 128).rearrange("p (a b) -> p a b", a=3)
        DVE(["cm"], ["cmb"]).tensor_copy(out=cmb[:, 0, :], in_=M_LE)
        DVE(["cm"], ["cmb"]).tensor_copy(out=cmb[:, 1, :], in_=M_GT)
        DVE(["cm"], ["cmb"]).tensor_copy(out=cmb[:, 2, :], in_=IDF)
        MB_LE, MB_GT, IDB = cmb[:, 0, :], cmb[:, 1, :], cmb[:, 2, :]
        ones_f = A.f32(128)
        DVE([], ["ones_f"]).memset(ones_f, 1.0)
        ones_b = A.bf16(128)
        DVE([], ["ones_b"]).memset(ones_b, 1.0)
        rope = A.f32(NT * 64).rearrange("p (i c) -> p i c", i=NT)
        LD(rope, dr["c_rope"], "rope")
        hnT = A.bf16(8 * T).rearrange("p (k t) -> p k t", k=8)
        HNBASE = A.top
        HT = A.bf16(4 * T).rearrange("p (m t) -> p m t", m=4)
        HBASE = A.top
        OT = A.bf16(4 * T).rearrange("p (m t) -> p m t", m=4)
        GT = A.bf16(4 * T).rearrange("p (m t) -> p m t", m=4)
        BASE = A.top
        S.barrier()
        HN_ALL = [("hnT", i) for i in range(NT)]

        def bc_load(dst, src_row, key):
            n = dst.shape[-1]
            LD(dst, src_row.unsqueeze(0).broadcast_to([128, n]), key)

        def load_w(dst, src, key):
            nk = dst.shape[1]
            for k in range(nk):
                S.dma("pool", dst[:, k, :], src[k * 128:(k + 1) * 128, :], str(key), writes=[key])

        def rstd_from_ss(ss, width, key):
            ACT([key], [key]).activation(out=ss, in_=ss, func=AF.Ln, scale=1.0 / width, bias=EPS)
            ACT([key], [key]).activation(out=ss, in_=ss, func=AF.Exp, scale=-0.5)

        class Ctx:
            pass

        def norm_setup(gain_row):
            n = Ctx()
            n.g = A.f32(D)
            bc_load(n.g, gain_row, "ng")
            n.junk = A.bf16(D)
            n.hn = [A.bf16(D), A.bf16(D)]
            n.ss = [A.f32(1), A.f32(1)]
            n.cnt = 0
            return n

        def norm_tile(n, h_ap, hkey, col0, dstT, dkey, psi):
            b = n.cnt % 2
            n.cnt += 1
            ss, hn = n.ss[b], n.hn[b]
            kss, khn = f"nss{b}", f"nhn{b}"
            DVE([hkey], ["njunk", kss]).scalar_tensor_tensor(out=n.junk, in0=h_ap, scalar=1.0, in1=h_ap, op0=ALU.mult,
                                                             op1=ALU.mult, accum_out=ss)
            rstd_from_ss(ss, float(D), kss)
            DVE([hkey, kss, "ng"], [khn]).scalar_tensor_tensor(out=hn, in0=h_ap, scalar=ss[:, 0:1], in1=n.g, op0=ALU.mult, op1=ALU.mult)
            for k in range(8):
                PE([khn, "cmb"], [PSK[psi]]).transpose(psb[psi][:, k * 128:(k + 1) * 128], hn[:, k * 128:(k + 1) * 128], IDB)
            ACT([PSK[psi]], [dkey]).activation(out=dstT[:, :, col0:col0 + 128],
                                               in_=psb[psi][:, 0:1024].rearrange("p (k t) -> p k t", k=8), func=AF.Copy)

        def norm_A(n, h_ap, hkey, i):
            b = i % 2
            DVE([hkey], ["njunk", f"nss{b}"]).scalar_tensor_tensor(out=n.junk, in0=h_ap, scalar=1.0, in1=h_ap, op0=ALU.mult,
                                                                   op1=ALU.mult, accum_out=n.ss[b])
            rstd_from_ss(n.ss[b], float(D), f"nss{b}")

        def norm_B(n, h_ap, hkey, i, dstT, dkey, psi):
            b = i % 2
            ss, hn = n.ss[b], n.hn[b]
            kss, khn = f"nss{b}", f"nhn{b}"
            DVE([hkey, kss, "ng"], [khn]).scalar_tensor_tensor(out=hn, in0=h_ap, scalar=ss[:, 0:1], in1=n.g, op0=ALU.mult, op1=ALU.mult)
            for k in range(8):
                PE([khn, "cmb"], [PSK[psi]]).transpose(psb[psi][:, k * 128:(k + 1) * 128], hn[:, k * 128:(k + 1) * 128], IDB)
            ACT([PSK[psi]], [dkey]).activation(out=dstT[:, :, i * 128:(i + 1) * 128],
                                               in_=psb[psi][:, 0:1024].rearrange("p (k t) -> p k t", k=8), func=AF.Copy)

        def headnorm(src, nh, hd, gains, dst, key_src, key_dst, tmp_sq, ss, kss):
            v3 = lambda a: a.rearrange("p (h d) -> p h d", h=nh)
            DVE([key_src], ["hsq"]).tensor_tensor(out=tmp_sq, in0=src, in1=src, op=ALU.mult)
            DVE(["hsq"], [kss]).tensor_reduce(out=ss, in_=v3(tmp_sq), axis=AX.X, op=ALU.add)
            rstd_from_ss(ss, float(hd), kss)
            DVE([key_src, kss], ["hsq"]).tensor_tensor(out=v3(tmp_sq), in0=v3(src), in1=ss.unsqueeze(2).broadcast_to([128, nh, hd]), op=ALU.mult)
            DVE(["hsq", "gains"], [key_dst]).tensor_tensor(out=dst, in0=tmp_sq, in1=gains, op=ALU.mult)

        def phase_N1():
            A.top = BASE
            n = norm_setup(dr["ln_mix"][0])
            hb = [A.f32(D), A.f32(D)]
            for i in range(NT):
                b = i % 2
                LD(hb[b], dr["x"][i * 128:(i + 1) * 128, :], f"hb{b}")
                norm_tile(n, hb[b], f"hb{b}", i * 128, hnT, ("hnT", i), 6 + b)
            S.barrier()

        def phase_A(l):
            w_in = dr["w_in"][l]
            A.top = BASE
            wqkv = A.bf16(8 * 768).rearrange("p (k n) -> p k n", k=8)
            for g in range(2):
                for k in range(8):
                    S.dma("pool", wqkv[:, k, 0:512].rearrange("p (h g d) -> p h g d", h=4, g=2)[:, :, g, :],
                          w_in[k * 128:(k + 1) * 128, 0:512].rearrange("p (g h d) -> p g h d", g=2, h=4)[:, g, :, :],
                          "wqkv", writes=["wqkv"])
            load_w(wqkv[:, :, 512:768], w_in[:, 512:768], "wqkv")
            gains = A.f32(640)
            gtmp = A.f32(128)
            bc_load(gtmp[:, 0:64], dr["attn_qnorm"][l], "gtmp")
            bc_load(gtmp[:, 64:128], dr["attn_knorm"][l], "gtmp")
            DVE(["gtmp"], ["gains"]).tensor_scalar(out=gains[:, 0:512].rearrange("p (h d) -> p h d", h=8),
                                                   in0=gtmp[:, 0:64].unsqueeze(1).broadcast_to([128, 8, 64]),
                                                   scalar1=0.125, scalar2=None, op0=ALU.mult)
            DVE(["gtmp"], ["gains"]).tensor_copy(out=gains[:, 512:640].rearrange("p (h d) -> p h d", h=2),
                                                 in_=gtmp[:, 64:128].unsqueeze(1).broadcast_to([128, 2, 64]))
            qT = A.bf16(4 * T).rearrange("p (m t) -> p m t", m=4)
            kTm = A.bf16(2 * T).rearrange("p (g t) -> p g t", g=2)
            POOL([], ["kT"]).memset(kTm, 0.0)
            Vaug = A.bf16(NT * 256).rearrange("p (i g c) -> p i g c", i=NT, g=2)
            POOL([], ["Vaug"]).memset(Vaug, 1.0)
            CORE_TOP = A.top
            qk = [A.f32(640), A.f32(640)]
            sq = A.f32(640)
            ssq = [A.f32(10), A.f32(10)]
            t1 = [A.f32(640), A.f32(640)]
            ra, rb_, rc, rd = A.f32(320), A.f32(320), A.f32(320), A.f32(320)
            rot = [A.bf16(640), A.bf16(640)]
            v4 = lambda a: a.rearrange("p (h a d) -> p h a d", h=10, a=2)
            WG_TOP = ARENA_W - 12288
            assert A.top <= WG_TOP, A.top

            def a1_mm(i):
                b = i % 2
                pa, pb = PSK[2 * b], PSK[2 * b + 1]
                tok = slice(i * 128, (i + 1) * 128)
                for k in range(8):
                    PE([("hnT", i), "wqkv"], [pa]).matmul(ps[2 * b][:, 0:512], hnT[:, k, tok], wqkv[:, k, 0:512], start=(k == 0), stop=(k == 7))
                for k in range(8):
                    PE([("hnT", i), "wqkv"], [pb]).matmul(ps[2 * b + 1][:, 0:256], hnT[:, k, tok], wqkv[:, k, 512:768], start=(k == 0), stop=(k == 7))

            v3h = lambda a: a.rearrange("p (h d) -> p h d", h=10)

            def a1_stA(i):
                b = i % 2
                pa, pb = PSK[2 * b], PSK[2 * b + 1]
                qkb, kq = qk[b], f"qk{b}"
                ACT([pa], [kq]).activation(out=qkb[:, 0:512], in_=ps[2 * b][:, 0:512], func=AF.Copy)
                ACT([pb], [kq]).activation(out=qkb[:, 512:640], in_=ps[2 * b + 1][:, 0:128], func=AF.Copy)
                ACT([pb], ["Vaug"]).activation(out=Vaug[:, i, :, 0:64], in_=ps[2 * b + 1][:, 128:256].rearrange("p (g c) -> p g c", g=2), func=AF.Copy)
                DVE([kq], ["hsq"]).tensor_tensor(out=sq, in0=qkb, in1=qkb, op=ALU.mult)
                DVE(["hsq"], [f"ssq{b}"]).tensor_reduce(out=ssq[b], in_=v3h(sq), axis=AX.X, op=ALU.add)
                rstd_from_ss(ssq[b], 64.0, f"ssq{b}")

            def a1_stB(i):
                b = i % 2
                tok = slice(i * 128, (i + 1) * 128)
                qkb, kq = qk[b], f"qk{b}"
                t1b, kt1 = t1[b], f"t1{b}"
                DVE([kq, f"ssq{b}"], [kt1]).tensor_tensor(out=v3h(t1b), in0=v3h(qkb), in1=ssq[b].unsqueeze(2).broadcast_to([128, 10, 64]), op=ALU.mult)
                DVE([kt1, "gains"], [kt1]).tensor_tensor(out=t1b, in0=t1b, in1=gains, op=ALU.mult)
                tv = t1b.rearrange("p (h a b d) -> p h a b d", h=10, a=2, b=2)
                x1, x2 = tv[:, :, :, 0, :], tv[:, :, :, 1, :]
                cosb = rope[:, i, 0:32].rearrange("p (a d) -> p a d", a=2).unsqueeze(1).broadcast_to([128, 10, 2, 16])
                sinb = rope[:, i, 32:64].rearrange("p (a d) -> p a d", a=2).unsqueeze(1).broadcast_to([128, 10, 2, 16])
                rv = rot[b].rearrange("p (h a b d) -> p h a b d", h=10, a=2, b=2)
                krot = f"rot{b}"
                DVE([kt1, "rope"], ["ra"]).tensor_tensor(out=v4(ra), in0=x1, in1=cosb, op=ALU.mult)
                DVE([kt1, "rope"], ["rb"]).tensor_tensor(out=v4(rb_), in0=x2, in1=sinb, op=ALU.mult)
                DVE(["ra", "rb"], [krot]).tensor_tensor(out=rv[:, :, :, 0, :], in0=v4(ra), in1=v4(rb_), op=ALU.subtract)
                POOL([kt1, "rope"], ["rc"]).tensor_tensor(out=v4(rc), in0=x1, in1=sinb, op=ALU.mult)
                POOL([kt1, "rope"], ["rd"]).tensor_tensor(out=v4(rd), in0=x2, in1=cosb, op=ALU.mult)
                POOL(["rc", "rd"], [krot]).tensor_tensor(out=rv[:, :, :, 1, :], in0=v4(rc), in1=v4(rd), op=ALU.add)
                pt = 4 + b
                for j in range(5):
                    PE([krot, "cmb"], [PSK[pt]]).transpose(psb[pt][:, j * 128:(j + 1) * 128], rot[b][:, j * 128:(j + 1) * 128], IDB)
                ACT([PSK[pt]], ["qT"]).activation(out=qT[:, :, tok], in_=psb[pt][:, 0:512].rearrange("p (m t) -> p m t", m=4), func=AF.Copy)
                ACT([PSK[pt]], ["kT"]).activation(out=kTm[0:64, 0, tok], in_=psb[pt][0:64, 512:640], func=AF.Copy)
                ACT([PSK[pt]], ["kT"]).activation(out=kTm[64:128, 1, tok], in_=psb[pt][64:128, 512:640], func=AF.Copy)

            a1_mm(0)
            a1_stA(0)
            a1_mm(1)
            for i in range(NT):
                if i + 1 < NT:
                    a1_stA(i + 1)
                if i + 2 < NT:
                    a1_mm(i + 2)
                a1_stB(i)
            S.barrier()
            if "M" in phases:
                wgall = arena_t[:, WG_TOP:ARENA_W].bitcast(BF16).rearrange("p (k n) -> p k n", k=8)
                for k in range(8):
                    for hf in range(2):
                        S.dma("pool", wgall[:, k, hf * 1536:(hf + 1) * 1536], w_in[k * 128:(k + 1) * 128, C_GATE + hf * 1536:C_GATE + (hf + 1) * 1536], "wgall", writes=["wgall"])
            A.top = CORE_TOP
            pT = [A.bf16(512) for _ in range(4)]
            rsb = [A.f32(512), A.f32(512)]
            steps = [(m, g, qb, kt) for m in range(4) for g in range(2) for qb in range(4) for kt in range(NT)]
            LOOK = 3

            def s_mm(s):
                m, g, qb, kt = steps[s]
                pr = slice(g * 64, (g + 1) * 64)
                bk = s % 4
                PE(["qT", "kT"], [PSK[bk]]).matmul(ps[bk][:, :], kTm[:, g, kt * 128:(kt + 1) * 128], qT[:, m, qb * 512:(qb + 1) * 512], start=True, stop=True)

            for s in range(LOOK):
                s_mm(s)
            for s in range(len(steps)):
                m, g, qb, kt = steps[s]
                it = s // NT
                bk = s % 4
                po = 4 + (it % 2)
                ACT([PSK[bk]], [f"pT{bk}"]).activation(out=pT[bk], in_=ps[bk][:, :], func=AF.Exp)
                if s + LOOK < len(steps):
                    s_mm(s + LOOK)
                PE([f"pT{bk}", "Vaug"], [PSK[po]]).matmul(ps[po][:, :], Vaug[:, kt, g, :], pT[bk], start=(kt == 0), stop=(kt == NT - 1))
                if kt == NT - 1:
                    pr = slice(g * 64, (g + 1) * 64)
                    qs = slice(qb * 512, (qb + 1) * 512)
                    rs = rsb[it % 2]
                    krs = f"rsb{it % 2}"
                    DVE([PSK[po]], [krs]).reciprocal(out=rs[0:64, :], in_=ps[po][64:128, :])
                    DVE([PSK[po], krs], ["OT"]).tensor_tensor(out=OT[pr, m, qs], in0=ps[po][0:64, :], in1=rs[0:64, :], op=ALU.mult)
            S.barrier(keep=["wgall"])

        def phase_G(l):
            w_in = dr["w_in"][l]
            A.top = BASE
            wg = A.bf16(8 * 1568).rearrange("p (k n) -> p k n", k=8)
            load_w(wg, w_in[:, C_GQ:C_GQ + 1568], "wg")
            glrT = A.f32(T)
            wblk = A.f32(512)
            DVE([], ["wblk"]).memset(wblk[0:64, :], 0.0)
            LD(wblk[0:16, 0:256], dr["gla_w_lr"][l, 0], "wblk")
            LD(wblk[16:32, 256:512], dr["gla_w_lr"][l, 1], "wblk")
            LD(wblk[32:33, :], dr["gla_b_lr"][l].rearrange("s d -> (s d)").unsqueeze(0), "wblk")
            DVE([], ["glrT"]).memset(glrT[32:64, :], 1.0)
            for tb in range(4):
                for k in range(8):
                    PE(HN_ALL + ["wg"], [PSK[tb]]).matmul(ps[tb][0:32, :], wg[:, k, 1536:1568], hnT[:, k, tb * 512:(tb + 1) * 512], start=(k == 0), stop=(k == 7))
                ACT([PSK[tb]], ["glrT"]).activation(out=glrT[0:32, tb * 512:(tb + 1) * 512], in_=ps[tb][0:32, :], func=AF.Copy)
            onb = A.f32(512)
            bc_load(onb[:, 0:128], dr["gla_onorm"][l], "onb0")
            DVE(["onb0"], ["gains"]).tensor_copy(out=onb[:, 128:512].rearrange("p (h d) -> p h d", h=3), in_=onb[:, 0:128].unsqueeze(1).broadcast_to([128, 3, 128]))
            osb = A.f32(NT * 512).rearrange("p (i c) -> p i c", i=NT)
            qdT = [A.bf16(256).rearrange("p (c t) -> p c t", c=2) for _ in range(2)]
            kiT = [A.bf16(256).rearrange("p (c t) -> p c t", c=2) for _ in range(2)]
            ke = [A.bf16(256) for _ in range(2)]
            vs = [A.bf16(512) for _ in range(2)]
            decs = [A.f32(4) for _ in range(2)]
            ee = A.f32(256)
            sp_ = A.f32(256)
            Ex = [A.f32(256) for _ in range(3)]
            stg = A.bf16(512)
            Sst = A.f32(256).rearrange("p (c e) -> p c e", c=2)
            Sbm = A.bf16(512).rearrange("p (c h e) -> p c h e", c=2, h=2)
            Am = A.bf16(512).rearrange("p (h t) -> p h t", h=4)
            MC = [(M_LE, M_NGT), (M_GE, M_NLT)]
            S.barrier()

            def g_tile(i, d, n):
                b = n % 2
                tok = slice(i * 128, (i + 1) * 128)
                for (bank, c0) in ((0, 0), (1, 512)):
                    for k in range(8):
                        PE([("hnT", i), "wg"], [PSK[bank]]).matmul(ps[bank][:, :], hnT[:, k, tok], wg[:, k, c0:c0 + 512], start=(k == 0), stop=(k == 7))
                PE(["glrT", "wblk"], [PSK[7]]).matmul(ps[7][:, 0:256], glrT[0:33, tok], wblk[0:33, d * 256:(d + 1) * 256], start=True, stop=True)
                ACT([PSK[7]], ["ee"]).activation(out=ee, in_=ps[7][:, 0:256], func=AF.Exp, scale=-1.0)
                ACT(["ee"], ["sp"]).activation(out=sp_, in_=ee, func=AF.Ln, bias=1.0, scale=1.0)
                ACT([PSK[1]], [f"vs{b}"]).activation(out=vs[b], in_=ps[1][:, :], func=AF.Copy)
                PE(["sp", "cm"], [PSK[4]]).matmul(ps[4][:, 0:256], MC[d][0], sp_, start=True, stop=True)
                PE(["sp", "cm"], [PSK[4]]).matmul(ps[4][:, 256:512], MC[d][1], sp_, start=True, stop=True)
                for c in range(2):
                    PE(["sp", "ones_f"], [PSK[3]]).matmul(ps[3][:, 256 + c * 2:256 + c * 2 + 2], sp_[:, c * 128:(c + 1) * 128], ones_f[:, 0:2], start=True, stop=True)
                ACT([PSK[3]], [f"decs{b}"]).activation(out=decs[b], in_=ps[3][:, 256:260], func=AF.Exp, scale=-1.0 / 16)
                E1, E2, E3 = Ex
                ACT([PSK[4]], ["E0"]).activation(out=E1, in_=ps[4][:, 0:256], func=AF.Exp, scale=-1.0 / 16)
                ACT([PSK[4]], ["E1"]).activation(out=E2, in_=ps[4][:, 0:256], func=AF.Exp, scale=1.0 / 16)
                ACT([PSK[4]], ["E2"]).activation(out=E3, in_=ps[4][:, 256:512], func=AF.Exp, scale=1.0 / 16)
                DVE(["E0", PSK[0]], ["stg"]).scalar_tensor_tensor(out=stg[:, 0:256], in0=E1, scalar=0.125, in1=ps[0][:, 0:256], op0=ALU.mult, op1=ALU.mult)
                DVE(["E1", PSK[0]], ["stg"]).tensor_tensor(out=stg[:, 256:512], in0=E2, in1=ps[0][:, 256:512], op=ALU.mult)
                DVE(["E2", PSK[0]], [f"ke{b}"]).tensor_tensor(out=ke[b], in0=E3, in1=ps[0][:, 256:512], op=ALU.mult)
                for j in range(4):
                    PE(["stg", "cmb"], [PSK[7]]).transpose(psb[7][:, 512 + j * 128:512 + (j + 1) * 128], stg[:, j * 128:(j + 1) * 128], IDB)
                pv = psb[7][:, 512:1024].rearrange("p (a c t) -> p a c t", a=2, c=2)
                ACT([PSK[7]], [f"qdT{b}"]).activation(out=qdT[b], in_=pv[:, 0, :, :], func=AF.Copy)
                ACT([PSK[7]], [f"kiT{b}"]).activation(out=kiT[b], in_=pv[:, 1, :, :], func=AF.Copy)

            def scan_tile(d, i, n, first):
                b = n % 2
                if first:
                    DVE([], ["Sst"]).memset(Sst, 0.0)
                    DVE([], ["Sb"]).memset(Sbm, 0.0)
                for h in range(4):
                    c, hh = h // 2, h % 2
                    pr = slice(hh * 64, (hh + 1) * 64)
                    bk = 5 if hh == 0 else 3
                    PE([f"qdT{b}", f"kiT{b}"], [PSK[bk]]).matmul(ps[bk][:, c * 128:(c + 1) * 128], kiT[b][pr, c, :], qdT[b][pr, c, :], start=True, stop=True)
                mask = MB_LE if d == 0 else MB_GT
                Amv = Am.rearrange("p (c h) t -> p c h t", c=2)
                for hh in range(2):
                    bk = 5 if hh == 0 else 3
                    DVE([PSK[bk], "cmb"], ["Am"]).tensor_tensor(out=Amv[:, :, hh, :], in0=ps[bk][:, 0:256].rearrange("p (c t) -> p c t", c=2),
                                                                in1=mask.unsqueeze(1).broadcast_to([128, 2, 128]), op=ALU.mult)
                for h in range(4):
                    c, hh = h // 2, h % 2
                    PE(["Am", f"vs{b}"], [PSK[2]]).matmul(ps[2][:, h * 128:(h + 1) * 128], Am[:, h, :], vs[b][:, h * 128:(h + 1) * 128], start=True, stop=False)
                    PE([f"qdT{b}", "Sb"], [PSK[2]]).matmul(ps[2][:, h * 128:(h + 1) * 128], qdT[b][:, c, :], Sbm[:, c, hh, :], start=False, stop=True)
                if d == 0:
                    ACT([PSK[2]], [("osb", i)]).activation(out=osb[:, i, :], in_=ps[2][:, :], func=AF.Copy)
                else:
                    DVE([PSK[2], ("osb", i)], [("osb", i)]).tensor_tensor(out=osb[:, i, :], in0=ps[2][:, :], in1=osb[:, i, :], op=ALU.add)
                for c in range(2):
                    PE([f"ke{b}", f"vs{b}"], [PSK[6]]).matmul(ps[6][:, c * 256:(c + 1) * 256], ke[b][:, c * 128:(c + 1) * 128], vs[b][:, c * 256:(c + 1) * 256], start=True, stop=True)
                for c in range(2):
                    for hh in range(2):
                        pr = slice(hh * 64, (hh + 1) * 64)
                        DVE(["Sst", f"decs{b}", PSK[6]], ["Sst"]).scalar_tensor_tensor(out=Sst[pr, c, :], in0=Sst[pr, c, :], scalar=decs[b][pr, 2 * c:2 * c + 1],
                                                                                       in1=ps[6][pr, c * 256 + hh * 128:c * 256 + (hh + 1) * 128], op0=ALU.mult, op1=ALU.add)
                for hh in range(2):
                    pr = slice(hh * 64, (hh + 1) * 64)
                    ACT(["Sst"], ["Sb"]).activation(out=Sbm[pr, :, hh, :], in_=Sst[pr, :, :], func=AF.Copy)

            seq = [(0, i) for i in range(NT)] + [(1, i) for i in range(NT - 1, -1, -1)]
            g_tile(seq[0][1], seq[0][0], 0)
            for n, (d, i) in enumerate(seq):
                if n + 1 < len(seq):
                    g_tile(seq[n + 1][1], seq[n + 1][0], n + 1)
                scan_tile(d, i, n, first=(i == (0 if d == 0 else NT - 1)))
            S.barrier()
            sq = A.f32(512)
            ss4 = [A.f32(4), A.f32(4)]
            yb = [A.bf16(512), A.bf16(512)]
            sgt = [A.bf16(512), A.bf16(512)]

            def fin_mm(i):
                b = i % 2
                for k in range(8):
                    PE([("hnT", i), "wg"], [PSK[b]]).matmul(ps[b][:, :], hnT[:, k, i * 128:(i + 1) * 128], wg[:, k, 1024:1536], start=(k == 0), stop=(k == 7))

            def fin_tile(i):
                b = i % 2
                ACT([PSK[b]], [f"sgt{b}"]).activation(out=sgt[b], in_=ps[b][:, :], func=AF.Silu)
                o = osb[:, i, :]
                DVE([("osb", i)], ["sq"]).tensor_tensor(out=sq, in0=o, in1=o, op=ALU.mult)
                DVE(["sq"], [f"ss4{b}"]).tensor_reduce(out=ss4[b], in_=sq.rearrange("p (h d) -> p h d", h=4), axis=AX.X, op=ALU.add)
                rstd_from_ss(ss4[b], 128.0, f"ss4{b}")
                DVE([("osb", i), f"ss4{b}"], ["sq"]).tensor_tensor(out=sq.rearrange("p (h d) -> p h d", h=4), in0=o.rearrange("p (h d) -> p h d", h=4),
                                                                   in1=ss4[b].unsqueeze(2).broadcast_to([128, 4, 128]), op=ALU.mult)
                DVE(["sq", "gains", "onb0"], ["sq"]).tensor_tensor(out=sq, in0=sq, in1=onb, op=ALU.mult)
                DVE(["sq", f"sgt{b}"], [f"yb{b}"]).tensor_tensor(out=yb[b], in0=sq, in1=sgt[b], op=ALU.mult)
                for j in range(4):
                    PE([f"yb{b}", "cmb"], [PSK[4 + b]]).transpose(psb[4 + b][:, j * 128:(j + 1) * 128], yb[b][:, j * 128:(j + 1) * 128], IDB)
                ACT([PSK[4 + b]], ["GT"]).activation(out=GT[:, :, i * 128:(i + 1) * 128], in_=psb[4 + b][:, 0:512].rearrange("p (m t) -> p m t", m=4), func=AF.Copy)

            fin_mm(0)
            for i in range(NT):
                if i + 1 < NT:
                    fin_mm(i + 1)
                fin_tile(i)
            S.barrier()

        def phase_H(l):
            w_in = dr["w_in"][l]
            A.top = HBASE
            w3 = A.bf16(2048)
            S.dma("pool", w3[0:64, :], dr["hy_w3"][l], "hw3", writes=["hw3"])
            S.dma("pool", w3[64:65, :], dr["hy_b3"][l].unsqueeze(0), "hw3", writes=["hw3"])
            cols = A.f32(4)
            LD(cols[0:64, 0:1], dr["hy_b1"][l].unsqueeze(1), "hcols")
            LD(cols[0:64, 1:2], dr["hy_b2"][l].unsqueeze(1), "hcols")
            LD(cols[0:64, 2:3], dr["hy_freq"][l, 0].unsqueeze(1), "hcols")
            LD(cols[0:64, 3:4], dr["hy_freq"][l, 1].unsqueeze(1), "hcols")
            H2 = A.bf16(T)
            DVE([], ["H2"]).memset(H2[64:96, :], 1.0)
            TOP1 = A.top
            zT = A.f32(T)
            LD(zT[0:33, :], dr["c_zT"], "zT")
            w1 = A.f32(64)
            LD(w1[0:33, :], dr["hy_w1"][l], "hw1")
            w2 = A.f32(64)
            LD(w2[0:64, :], dr["hy_w2"][l], "hw2")
            H1 = A.f32(T)
            arg = A.f32(T)
            for layer in range(2):
                src_, wt, kk, dst, bcol, fcol = (zT, w1, 33, H1, 0, 2) if layer == 0 else (H1, w2, 64, H2, 1, 3)
                dk = "H1" if layer == 0 else "H2"
                for tb in range(4):
                    PE(["zT", "hw1", "hw2", "H1"], [PSK[tb]]).matmul(ps[tb][0:64, :], wt[0:kk, :], src_[0:kk, tb * 512:(tb + 1) * 512], start=True, stop=True)
                    ACT([PSK[tb], "hcols"], ["arg"]).activation(out=arg[0:64, tb * 512:(tb + 1) * 512], in_=ps[tb][0:64, :], func=AF.Identity,
                                                               bias=cols[0:64, bcol:bcol + 1], scale=1.0)
                    ACT(["arg", "hcols"], ["arg"]).activation(out=arg[0:64, tb * 512:(tb + 1) * 512], in_=arg[0:64, tb * 512:(tb + 1) * 512], func=AF.Copy,
                                                             scale=cols[0:64, fcol:fcol + 1])
                a64, m64 = arg[0:64, :], dst[0:64, :]
                DVE(["arg"], [dk]).tensor_single_scalar(out=m64, in_=a64, scalar=math.pi, op=ALU.is_gt)
                DVE(["arg", dk], ["arg"]).scalar_tensor_tensor(out=a64, in0=m64, scalar=-2.0 * math.pi, in1=a64, op0=ALU.mult, op1=ALU.add)
                DVE(["arg"], [dk]).tensor_single_scalar(out=m64, in_=a64, scalar=-math.pi, op=ALU.is_lt)
                DVE(["arg", dk], ["arg"]).scalar_tensor_tensor(out=a64, in0=m64, scalar=2.0 * math.pi, in1=a64, op0=ALU.mult, op1=ALU.add)
                ACT(["arg"], [dk]).activation(out=m64, in_=a64, func=AF.Sin)
            S.barrier()
            A.top = TOP1
            fsd = A.bf16(NT * 2 * 2 * 512).rearrange("p (i a o c) -> p i a o c", i=NT, a=2, o=2)
            filt_f = A.f32(2048)
            absf_f = A.bf16(2048)
            filt = filt_f.rearrange("p (o s c) -> p o s c", o=2, s=2)
            absf = absf_f.rearrange("p (o s c) -> p o s c", o=2, s=2)
            win = [A.f32(512), A.f32(512)]
            rn = A.f32(1024).rearrange("p (o c) -> p o c", o=2)

            def h0_tile(i):
                b = i % 2
                LD(win[b], dr["c_win"][i * 128:(i + 1) * 128, :], f"win{b}")
                for cb in range(4):
                    PE(["H2", "hw3"], [PSK[cb]]).matmul(ps[cb][:, :], H2[0:65, i * 128:(i + 1) * 128], w3[0:65, cb * 512:(cb + 1) * 512], start=True, stop=True)
                    DVE([PSK[cb], f"win{b}"], ["filt"]).tensor_tensor(out=filt[:, cb // 2, cb % 2, :], in0=ps[cb][:, :], in1=win[b], op=ALU.mult)
                if i == 0:
                    DVE(["filt"], ["filt"]).memset(filt[0:1, :, 1, :], 0.0)
                ACT(["filt"], ["absf"]).activation(out=absf_f, in_=filt_f, func=AF.Abs)
                for o in range(2):
                    for s_ in range(2):
                        PE(["absf", "ones_b"], [PSK[4 + o]]).matmul(ps[4 + o][:, :], ones_b, absf[:, o, s_, :], start=(i == 0 and s_ == 0), stop=(i == NT - 1 and s_ == 1))
                DVE(["filt"], ["fsd"]).tensor_tensor(out=fsd[:, i, 0, :, :], in0=filt[:, :, 0, :], in1=filt[:, :, 1, :], op=ALU.add)
                POOL(["filt"], ["fsd"]).tensor_tensor(out=fsd[:, i, 1, :, :], in0=filt[:, :, 0, :], in1=filt[:, :, 1, :], op=ALU.subtract)

            for i in range(NT):
                h0_tile(i)
            for o in range(2):
                DVE([PSK[4 + o]], ["rn"]).tensor_scalar(out=rn[:, o, :], in0=ps[4 + o][:, :], scalar1=EPS, scalar2=None, op0=ALU.add)
            DVE(["rn"], ["rn"]).reciprocal(out=rn, in_=rn)
            S.barrier()
            ftb = [A.bf16(2 * NT * 128).rearrange("p (a i f) -> p a i f", a=2, i=NT) for _ in range(2)]
            spo = [A.f32(2048).rearrange("p (o a c) -> p o a c", o=2, a=2) for _ in range(2)]

            def h0_ld(j):
                b = j % 2
                LD(ftb[b][:, 0, :, :], dr["c_ft"][j, 0], f"ft{b}")
                LD(ftb[b][:, 1, :, :], dr["c_ft"][j, 1], f"ft{b}")

            def h0_spec(j):
                b = j % 2
                if j + 1 < 16:
                    h0_ld(j + 1)
                for o in range(2):
                    for a in range(2):
                        bank = (j % 2) * 4 + o * 2 + a
                        for i in range(NT):
                            PE([f"ft{b}", "fsd"], [PSK[bank]]).matmul(ps[bank][:, :], ftb[b][:, a, i, :], fsd[:, i, a, o, :], start=(i == 0), stop=(i == NT - 1))
                        DVE([PSK[bank], "rn"], [f"spo{b}"]).tensor_tensor(out=spo[b][:, o, a, :], in0=ps[bank][:, :], in1=rn[:, o, :], op=ALU.mult)
                for o in range(2):
                    ST(specD[o, j].rearrange("a p c -> p a c"), spo[b][:, o, :, :], f"spo{b}", ("specD", o, j))

            h0_ld(0)
            for j in range(16):
                h0_spec(j)
            S.barrier()
            A.top = HBASE
            vT = A.bf16(4 * T).rearrange("p (m t) -> p m t", m=4)
            vt = A.bf16(NT * 512).rearrange("p (i c) -> p i c", i=NT)
            zTt = A.bf16(4 * T).rearrange("p (m t) -> p m t", m=4)
            skipc = A.f32(8).rearrange("p (o m) -> p o m", o=2)
            for o in range(2):
                for m in range(4):
                    LD(skipc[:, o, m:m + 1], dr["hy_skip"][l, o, m * 128:(m + 1) * 128].unsqueeze(1), "skipc")
            H1TOP = A.top
            whb = A.bf16(8 * 1536).rearrange("p (k n) -> p k n", k=8)
            load_w(whb, w_in[:, C_HY:C_HY + 1536], "whb")
            cw = A.f32(36).rearrange("p (c k) -> p c k", c=12)
            for cc in range(12):
                for k in range(3):
                    LD(cw[:, cc, k:k + 1], dr["hy_conv"][l, k, cc * 128:(cc + 1) * 128].unsqueeze(1), "cw")
            upad = [A.f32(T + 2), A.f32(T + 2)]
            for b in range(2):
                DVE([], [f"upad{b}"]).memset(upad[b], 0.0)
            ctmp = [A.f32(T), A.f32(T)]
            xo = [A.bf16(T), A.bf16(T)]

            def h1_chunk(cc):
                b = cc % 2
                for tb in range(4):
                    bank = (cc % 2) * 4 + tb
                    for k in range(8):
                        PE(HN_ALL + ["whb"], [PSK[bank]]).matmul(ps[bank][:, :], whb[:, k, cc * 128:(cc + 1) * 128], hnT[:, k, tb * 512:(tb + 1) * 512], start=(k == 0), stop=(k == 7))
                    ACT([PSK[bank]], [f"upad{b}"]).activation(out=upad[b][:, 1 + tb * 512:1 + (tb + 1) * 512], in_=ps[bank][:, :], func=AF.Copy)
                u = upad[b]
                E = DVE if b == 0 else POOL
                ACT([f"upad{b}", "cw"], [f"ctmp{b}"]).activation(out=ctmp[b], in_=u[:, 1:T + 1], func=AF.Copy, scale=cw[:, cc, 1:2])
                DVE([f"upad{b}", "cw", f"ctmp{b}"], [f"ctmp{b}"]).scalar_tensor_tensor(out=ctmp[b], in0=u[:, 0:T], scalar=cw[:, cc, 0:1], in1=ctmp[b], op0=ALU.mult, op1=ALU.add)
                if cc < 4:
                    DVE([f"upad{b}", "cw", f"ctmp{b}"], ["vT"]).scalar_tensor_tensor(out=vT[:, cc, :], in0=u[:, 2:T + 2], scalar=cw[:, cc, 2:3], in1=ctmp[b], op0=ALU.mult, op1=ALU.add)
                else:
                    DVE([f"upad{b}", "cw", f"ctmp{b}"], [f"xo{b}"]).scalar_tensor_tensor(out=xo[b], in0=u[:, 2:T + 2], scalar=cw[:, cc, 2:3], in1=ctmp[b], op0=ALU.mult, op1=ALU.add)
                    ST(hyD[cc - 4], xo[b], f"xo{b}", ("hyD", cc - 4))

            for cc in range(12):
                h1_chunk(cc)
            S.barrier()
            A.top = H1TOP
            Y = A.bf16(32 * 512).rearrange("p (j c) -> p j c", j=32)
            ftb = [A.bf16(2 * NT * 128).rearrange("p (a i f) -> p a i f", a=2, i=NT) for _ in range(2)]
            spb = [A.f32(1024).rearrange("p (a c) -> p a c", a=2) for _ in range(2)]
            gbuf = A.bf16(32 * 512).rearrange("p (j t) -> p j t", j=32)
            tm = [A.f32(512) for _ in range(4)]
            xb = [A.bf16(512), A.bf16(512)]
            ytmp = [A.f32(512), A.f32(512)]

            def to_tokmajor(srcT, skey, dkey):
                for i in range(NT):
                    bank = 6 + (i % 2)
                    for m in range(4):
                        PE([skey, "cmb"], [PSK[bank]]).transpose(psb[bank][:, m * 128:(m + 1) * 128], srcT[:, m, i * 128:(i + 1) * 128], IDB)
                    ACT([PSK[bank]], [dkey]).activation(out=vt[:, i, :], in_=psb[bank][:, 0:512], func=AF.Copy)

            def longconv(o, srcT, skey, xbase, dstT, dkey):
                def fwd(j):
                    b = j % 2
                    LD(ftb[b][:, 0, :, :], dr["c_ft"][j, 0], f"ft{b}")
                    LD(ftb[b][:, 1, :, :], dr["c_ft"][j, 1], f"ft{b}")
                    LD(spb[b], specD[o, j].rearrange("a p c -> p a c"), f"spb{b}", reads=[("specD", o, j)])
                    for a in range(2):
                        bank = b * 2 + a
                        for i in range(NT):
                            PE([f"ft{b}", "vt"], [PSK[bank]]).matmul(ps[bank][:, :], ftb[b][:, a, i, :], vt[:, i, :], start=(i == 0), stop=(i == NT - 1))
                    pr, pi = ps[b * 2][:, :], ps[b * 2 + 1][:, :]
                    fr, fi = spb[b][:, 0, :], spb[b][:, 1, :]
                    kr, ki = PSK[b * 2], PSK[b * 2 + 1]
                    DVE([kr, f"spb{b}"], ["tm0"]).tensor_tensor(out=tm[0], in0=pr, in1=fr, op=ALU.mult)
                    DVE([ki, f"spb{b}"], ["tm1"]).tensor_tensor(out=tm[1], in0=pi, in1=fi, op=ALU.mult)
                    DVE([kr, f"spb{b}"], ["tm2"]).tensor_tensor(out=tm[2], in0=pr, in1=fi, op=ALU.mult)
                    DVE([ki, f"spb{b}"], ["tm3"]).tensor_tensor(out=tm[3], in0=pi, in1=fr, op=ALU.mult)
                    POOL(["tm0", "tm1"], ["Y"]).tensor_tensor(out=Y[:, j, :], in0=tm[0], in1=tm[1], op=ALU.subtract)
                    POOL(["tm2", "tm3"], ["Y"]).tensor_tensor(out=Y[:, 16 + j, :], in0=tm[2], in1=tm[3], op=ALU.add)

                for j in range(16):
                    fwd(j)

                def inv(tb, it0):
                    for q4 in range(4):
                        LD(gbuf[:, q4 * 8:(q4 + 1) * 8, :], dr["c_g"][tb, :, q4 * 8:(q4 + 1) * 8, :], ("gbuf", q4))
                    for cc in range(4):
                        it = it0 + cc
                        bank = 4 + (it % 2)
                        b = it % 2
                        LD(xb[b], hyD[xbase + cc][:, tb * 512:(tb + 1) * 512], f"xb{b}", reads=[("hyD", xbase + cc)])
                        for jj in range(32):
                            PE(["Y", ("gbuf", jj // 8)], [PSK[bank]]).matmul(ps[bank][:, :], Y[:, jj, cc * 128:(cc + 1) * 128], gbuf[:, jj, :], start=(jj == 0), stop=(jj == 31))
                        DVE([skey, "skipc", PSK[bank]], [f"ytmp{b}"]).scalar_tensor_tensor(out=ytmp[b], in0=srcT[:, cc, tb * 512:(tb + 1) * 512], scalar=skipc[:, o, cc:cc + 1],
                                                                                         in1=ps[bank][:, :], op0=ALU.mult, op1=ALU.add)
                        DVE([f"ytmp{b}", f"xb{b}"], [dkey]).tensor_tensor(out=dstT[:, cc, tb * 512:(tb + 1) * 512], in0=ytmp[b], in1=xb[b], op=ALU.mult)

                for tb in range(4):
                    inv(tb, tb * 4)

            to_tokmajor(vT, "vT", "vt")
            longconv(0, vT, "vT", 0, zTt, "zTt")
            to_tokmajor(zTt, "zTt", "vt")
            longconv(1, zTt, "zTt", 4, HT, "HT")
            S.barrier()

        def resid_mm(i, src_h, hb, hkey, pbanks, lhs_fn, rhs_fn, nk, rkeys):
            LD(hb, src_h[i * 128:(i + 1) * 128, :], hkey, reads=[("hD", i)])
            for half in range(2):
                bank = pbanks[half]
                for k in range(nk):
                    PE(rkeys, [PSK[bank]]).matmul(ps[bank][:, :], lhs_fn(k), rhs_fn(k, half), start=(k == 0), stop=(k == nk - 1))

        def resid_add(i, hb, hkey, pbanks, dst_h):
            for half in range(2):
                bank = pbanks[half]
                DVE([PSK[bank], hkey], [hkey]).tensor_tensor(out=hb[:, half * 512:(half + 1) * 512], in0=ps[bank][:, :], in1=hb[:, half * 512:(half + 1) * 512], op=ALU.add)
            ST(dst_h[i * 128:(i + 1) * 128, :], hb, hkey, ("hD", i))

        def resid_loop(src_h, hb, lhs_of, rhs_fn, rkeys, dst_h, n, nbufs=3):
            def mm(i):
                b = i % nbufs
                resid_mm(i, src_h, hb[b], f"hb{b}", (2 * (i % 2), 2 * (i % 2) + 1), lhs_of(i), rhs_fn, 8, rkeys)
            mm(0)
            for i in range(NT):
                if i + 1 < NT:
                    mm(i + 1)
                b = i % nbufs
                resid_add(i, hb[b], f"hb{b}", (2 * (i % 2), 2 * (i % 2) + 1), dst_h)
                norm_A(n, hb[b], f"hb{b}", i)
                if i >= 1:
                    pb_ = (i - 1) % nbufs
                    norm_B(n, hb[pb_], f"hb{pb_}", i - 1, hnT, ("hnT", i - 1), 6 + ((i - 1) % 2))
            pb_ = (NT - 1) % nbufs
            norm_B(n, hb[pb_], f"hb{pb_}", NT - 1, hnT, ("hnT", NT - 1), 6 + ((NT - 1) % 2))

        def phase_M(l):
            w_in = dr["w_in"][l]
            A.top = BASE
            mixT = A.bf16(8 * T).rearrange("p (k t) -> p k t", k=8)
            MTOP = A.top
            wbr = A.bf16(3 * 4 * D).rearrange("p (b m n) -> p b m n", b=3, m=4)
            for m in range(4):
                for g in range(2):
                    r0 = (g * 4 + m) * 64
                    S.dma("pool", wbr[g * 64:(g + 1) * 64, 0, m, :], dr["w_br_attn"][l, r0:r0 + 64, :], "wbr", writes=["wbr"])
            load_w(wbr[:, 1, :, :], dr["w_br_hyena"][l], "wbr")
            load_w(wbr[:, 2, :, :], dr["w_br_gla"][l], "wbr")
            WG_TOP = ARENA_W - 12288
            wgall = arena_t[:, WG_TOP:ARENA_W].bitcast(BF16).rearrange("p (k n) -> p k n", k=8)
            if "A" not in phases:
                for k in range(8):
                    for hf in range(2):
                        S.dma("pool", wgall[:, k, hf * 1536:(hf + 1) * 1536], w_in[k * 128:(k + 1) * 128, C_GATE + hf * 1536:C_GATE + (hf + 1) * 1536], "wgall", writes=["wgall"])
            sig = [A.f32(512), A.f32(512)]
            tt = [A.f32(512), A.f32(512)]
            acc = A.f32(512)
            assert A.top <= WG_TOP, A.top
            brT = [OT, HT, GT]
            brK = ["OT", "HT", "GT"]
            stp = Ctx()
            stp.n = 0

            def m_iter(oc, tb):
                tk = slice(tb * 512, (tb + 1) * 512)
                wk = "wgall"
                for b in range(3):
                    r = stp.n % 4
                    stp.n += 1
                    by, bg = 2 * r, 2 * r + 1
                    for m in range(4):
                        PE(["wbr", brK[b]], [PSK[by]]).matmul(ps[by][:, :], wbr[:, b, m, oc * 128:(oc + 1) * 128], brT[b][:, m, tk], start=(m == 0), stop=(m == 3))
                    for k in range(8):
                        PE(HN_ALL + [wk], [PSK[bg]]).matmul(ps[bg][:, :], wgall[:, k, b * D + oc * 128:b * D + (oc + 1) * 128], hnT[:, k, tk], start=(k == 0), stop=(k == 7))
                    sb = sig[r % 2]
                    ACT([PSK[bg]], [f"sig{r % 2}"]).activation(out=sb, in_=ps[bg][:, :], func=AF.Sigmoid)
                    if b == 0:
                        DVE([PSK[by], f"sig{r % 2}"], ["acc"]).tensor_tensor(out=acc, in0=ps[by][:, :], in1=sb, op=ALU.mult)
                    else:
                        DVE([PSK[by], f"sig{r % 2}"], [f"tt{r % 2}"]).tensor_tensor(out=tt[r % 2], in0=ps[by][:, :], in1=sb, op=ALU.mult)
                        if b == 1:
                            POOL(["acc", f"tt{r % 2}"], ["acc"]).tensor_tensor(out=acc, in0=acc, in1=tt[r % 2], op=ALU.add)
                        else:
                            POOL(["acc", f"tt{r % 2}"], ["mixT"]).tensor_tensor(out=mixT[:, oc, tk], in0=acc, in1=tt[r % 2], op=ALU.add)

            for oc in range(8):
                for tb in range(4):
                    m_iter(oc, tb)
            S.barrier()
            A.top = MTOP
            wout = A.bf16(8 * D).rearrange("p (k n) -> p k n", k=8)
            load_w(wout, dr["w_out"][l], "wout")
            n = norm_setup(dr["ln_x"][l])
            hb = [A.f32(D), A.f32(D), A.f32(D)]
            resid_loop((dr["x"] if l == 0 else hbuf), hb, lambda i: (lambda k: mixT[:, k, i * 128:(i + 1) * 128]),
                       lambda k, half: wout[:, k, half * 512:(half + 1) * 512], ["mixT", "wout"], hbuf, n)
            S.barrier()

        def phase_X(l):
            W1TOP = ARENA_W - 16384
            A.top = HNBASE
            kxT = A.bf16(8 * MEM).rearrange("p (k t) -> p k t", k=8)
            vx = A.bf16(2 * D).rearrange("p (i c) -> p i c", i=2)
            gains = A.f32(D)
            gtmp = A.f32(256)
            sq = A.f32(D)
            qsb = [A.f32(D), A.f32(D)]
            nrm = [A.bf16(D), A.bf16(D)]
            ss4 = [A.f32(4), A.f32(4)]
            XP = A.top
            wk = A.bf16(8 * D).rearrange("p (k n) -> p k n", k=8)
            wv = A.bf16(8 * D).rearrange("p (k n) -> p k n", k=8)
            load_w(wk, dr["x_wk"][l], "wk")
            load_w(wv, dr["x_wv"][l], "wv")
            mnT = A.bf16(8 * MEM).rearrange("p (k t) -> p k t", k=8)
            n = norm_setup(dr["ln_mem"][l])
            mb = [A.f32(D), A.f32(D)]
            assert A.top <= W1TOP
            bc_load(gtmp, dr["x_knorm"][l], "gtmp")
            DVE(["gtmp"], ["gains"]).tensor_copy(out=gains.rearrange("p (h d) -> p h d", h=4), in_=gtmp.unsqueeze(1).broadcast_to([128, 4, 256]))
            for mt in range(2):
                LD(mb[mt], dr["mem"][mt * 128:(mt + 1) * 128, :], f"mb{mt}")
                norm_tile(n, mb[mt], f"mb{mt}", mt * 128, mnT, "mnT", 6 + mt)
            for mt in range(2):
                tok = slice(mt * 128, (mt + 1) * 128)
                for half in range(2):
                    for k in range(8):
                        PE(["mnT", "wk"], [PSK[half]]).matmul(ps[half][:, :], mnT[:, k, tok], wk[:, k, half * 512:(half + 1) * 512], start=(k == 0), stop=(k == 7))
                    ACT([PSK[half]], [f"qsb{mt}"]).activation(out=qsb[mt][:, half * 512:(half + 1) * 512], in_=ps[half][:, :], func=AF.Copy)
                    for k in range(8):
                        PE(["mnT", "wv"], [PSK[2 + half]]).matmul(ps[2 + half][:, :], mnT[:, k, tok], wv[:, k, half * 512:(half + 1) * 512], start=(k == 0), stop=(k == 7))
                    ACT([PSK[2 + half]], ["vx"]).activation(out=vx[:, mt, half * 512:(half + 1) * 512], in_=ps[2 + half][:, :], func=AF.Copy)
                headnorm(qsb[mt], 4, 256, gains, nrm[mt], f"qsb{mt}", f"nrm{mt}", sq, ss4[mt], f"ss4{mt}")
                for j in range(8):
                    PE([f"nrm{mt}", "cmb"], [PSK[4 + mt]]).transpose(psb[4 + mt][:, j * 128:(j + 1) * 128], nrm[mt][:, j * 128:(j + 1) * 128], IDB)
                ACT([PSK[4 + mt]], ["kxT"]).activation(out=kxT[:, :, tok], in_=psb[4 + mt][:, 0:1024].rearrange("p (k t) -> p k t", k=8), func=AF.Copy)
            S.barrier()
            A.top = XP
            qxT = A.bf16(8 * T).rearrange("p (k t) -> p k t", k=8)
            wq = A.bf16(8 * D).rearrange("p (k n) -> p k n", k=8)
            load_w(wq, dr["x_wq"][l], "wq")
            w1 = arena_t[:, W1TOP:ARENA_W].bitcast(BF16).rearrange("p (k n) -> p k n", k=8)
            for q4 in range(4):
                load_w(w1[:, :, q4 * D:(q4 + 1) * D], dr["mlp_w1"][l][:, q4 * D:(q4 + 1) * D], "w1")
            bc_load(gtmp, dr["x_qnorm"][l], "gtmp")
            DVE(["gtmp"], ["gains"]).tensor_scalar(out=gains.rearrange("p (h d) -> p h d", h=4), in0=gtmp.unsqueeze(1).broadcast_to([128, 4, 256]),
                                                   scalar1=1.0 / 16, scalar2=None, op0=ALU.mult)

            def xq_mm(i):
                b = i % 2
                tok = slice(i * 128, (i + 1) * 128)
                for half in range(2):
                    bank = 2 * b + half
                    for k in range(8):
                        PE([("hnT", i), "wq"], [PSK[bank]]).matmul(ps[bank][:, :], hnT[:, k, tok], wq[:, k, half * 512:(half + 1) * 512], start=(k == 0), stop=(k == 7))

            sqB = A.f32(D)
            v3x = lambda a: a.rearrange("p (h d) -> p h d", h=4)

            def xq_stA(i):
                b = i % 2
                for half in range(2):
                    bank = 2 * b + half
                    ACT([PSK[bank]], [f"qsb{b}"]).activation(out=qsb[b][:, half * 512:(half + 1) * 512], in_=ps[bank][:, :], func=AF.Copy)
                DVE([f"qsb{b}"], ["hsq"]).tensor_tensor(out=sq, in0=qsb[b], in1=qsb[b], op=ALU.mult)
                DVE(["hsq"], [f"ss4{b}"]).tensor_reduce(out=ss4[b], in_=v3x(sq), axis=AX.X, op=ALU.add)
                rstd_from_ss(ss4[b], 256.0, f"ss4{b}")

            def xq_stB(i):
                b = i % 2
                tok = slice(i * 128, (i + 1) * 128)
                DVE([f"qsb{b}", f"ss4{b}"], ["sqB"]).tensor_tensor(out=v3x(sqB), in0=v3x(qsb[b]), in1=ss4[b].unsqueeze(2).broadcast_to([128, 4, 256]), op=ALU.mult)
                DVE(["sqB", "gains"], [f"nrm{b}"]).tensor_tensor(out=nrm[b], in0=sqB, in1=gains, op=ALU.mult)
                for j in range(8):
                    PE([f"nrm{b}", "cmb"], [PSK[4 + b]]).transpose(psb[4 + b][:, j * 128:(j + 1) * 128], nrm[b][:, j * 128:(j + 1) * 128], IDB)
                ACT([PSK[4 + b]], [("qx", c, i // 4) for c in range(8)]).activation(out=qxT[:, :, tok], in_=psb[4 + b][:, 0:1024].rearrange("p (k t) -> p k t", k=8), func=AF.Copy)

            xq_mm(0)
            xq_stA(0)
            xq_mm(1)
            for i in range(NT):
                if i + 1 < NT:
                    xq_stA(i + 1)
                if i + 2 < NT:
                    xq_mm(i + 2)
                xq_stB(i)
            S.barrier(keep=["w1"])
            A.top = XP + 8192
            wo = A.bf16(8 * D).rearrange("p (k n) -> p k n", k=8)
            load_w(wo, dr["x_wo"][l], "wo")
            pT = [A.bf16(512) for _ in range(4)]
            rden = [A.f32(512), A.f32(512)]
            lnd = [A.f32(512), A.f32(512)]

            def xc_scores(hd, qb, it):
                qs = slice(qb * 512, (qb + 1) * 512)
                b = it % 2
                for mt in range(2):
                    bank = 4 * b + mt
                    for half in range(2):
                        PE(["kxT", ("qx", hd * 2 + half, qb)], [PSK[bank]]).matmul(ps[bank][:, :], kxT[:, hd * 2 + half, mt * 128:(mt + 1) * 128], qxT[:, hd * 2 + half, qs],
                                                                                 start=(half == 0), stop=(half == 1))
                    ACT([PSK[bank]], [f"pT{2 * b + mt}"]).activation(out=pT[2 * b + mt], in_=ps[bank][:, :], func=AF.Exp, bias=-8.0, scale=1.0)

            def xc_out(hd, qb, it):
                qs = slice(qb * 512, (qb + 1) * 512)
                b = it % 2
                bden = 4 * b + 2
                for mt in range(2):
                    PE([f"pT{2 * b + mt}", "ones_b"], [PSK[bden]]).matmul(ps[bden][:, :], ones_b, pT[2 * b + mt], start=(mt == 0), stop=(mt == 1))
                ACT([PSK[bden]], [f"lnd{b}"]).activation(out=lnd[b], in_=ps[bden][:, :], func=AF.Ln)
                ACT([f"lnd{b}"], [f"rden{b}"]).activation(out=rden[b], in_=lnd[b], func=AF.Exp, scale=-1.0)
                for half in range(2):
                    bo = 4 * b + 3 if half == 0 else 4 * b
                    for mt in range(2):
                        PE([f"pT{2 * b}", f"pT{2 * b + 1}", "vx"], [PSK[bo]]).matmul(ps[bo][:, :], vx[:, mt, hd * 256 + half * 128:hd * 256 + (half + 1) * 128], pT[2 * b + mt],
                                                                                  start=(mt == 0), stop=(mt == 1))
                    DVE([PSK[bo], f"rden{b}"], [("qx", hd * 2 + half, qb)]).tensor_tensor(out=qxT[:, hd * 2 + half, qs], in0=ps[bo][:, :], in1=rden[b], op=ALU.mult)

            its = [(hd, qb) for hd in range(4) for qb in range(4)]
            xc_scores(its[0][0], its[0][1], 0)
            for it, (hd, qb) in enumerate(its):
                if it + 1 < len(its):
                    xc_scores(its[it + 1][0], its[it + 1][1], it + 1)
                xc_out(hd, qb, it)
            n = norm_setup(dr["ln_mlp"][l])
            assert A.top <= W1TOP, A.top
            hb = [qsb[0], qsb[1], sq]
            QX_ALL = [("qx", c, qb) for c in range(8) for qb in range(4)]
            S.barrier(keep=["w1"])
            resid_loop(hbuf, hb, lambda i: (lambda k: qxT[:, k, i * 128:(i + 1) * 128]),
                       lambda k, half: wo[:, k, half * 512:(half + 1) * 512], ["wo"] + QX_ALL, hbuf, n)
            S.barrier(keep=["w1"])

        def phase_F(l, last):
            A.top = HNBASE
            W1TOP = ARENA_W - 16384
            w1 = arena_t[:, W1TOP:ARENA_W].bitcast(BF16).rearrange("p (k n) -> p k n", k=8)
            if "X" not in phases:
                for q4 in range(4):
                    load_w(w1[:, :, q4 * D:(q4 + 1) * D], dr["mlp_w1"][l][:, q4 * D:(q4 + 1) * D], "w1")
            w2 = A.bf16(32 * D).rearrange("p (j n) -> p j n", j=32)
            load_w(w2, dr["mlp_w2"][l], "w2")
            hid = A.bf16(16 * 256).rearrange("p (j t) -> p j t", j=16)
            rl = [A.bf16(256), A.bf16(256)]
            hb = [A.f32(D), A.f32(D)]
            n = None if last else norm_setup(dr["ln_mix"][l + 1])
            dst = out if last else hbuf
            stp = Ctx()
            stp.n = 0

            def f_block(tb):
                tk = slice(tb * 256, (tb + 1) * 256)
                for fh in range(2):
                    for j in range(16):
                        jj = fh * 16 + j
                        bank = 4 + (stp.n % 4)
                        r = stp.n % 2
                        stp.n += 1
                        for k in range(8):
                            PE([("hnT", 2 * tb), ("hnT", 2 * tb + 1), "w1"], [PSK[bank]]).matmul(ps[bank][:, 0:256], w1[:, k, jj * 128:(jj + 1) * 128], hnT[:, k, tk], start=(k == 0), stop=(k == 7))
                        ACT([PSK[bank]], [f"rl{r}"]).activation(out=rl[r], in_=ps[bank][:, 0:256], func=AF.Relu)
                        POOL([f"rl{r}"], [("hid", j)]).tensor_tensor(out=hid[:, j, :], in0=rl[r], in1=rl[r], op=ALU.mult)
                    for tt_ in range(2):
                        for half in range(2):
                            bank = tt_ * 2 + half
                            for j in range(16):
                                jj = fh * 16 + j
                                PE([("hid", j), "w2"], [PSK[bank]]).matmul(ps[bank][:, :], hid[:, j, tt_ * 128:(tt_ + 1) * 128], w2[:, jj, half * 512:(half + 1) * 512],
                                                                           start=(fh == 0 and j == 0), stop=(fh == 1 and j == 15))
                for tt_ in range(2):
                    i = tb * 2 + tt_
                    b = i % 2
                    LD(hb[b], hbuf[i * 128:(i + 1) * 128, :], f"hb{b}", reads=[("hD", i)])
                    for half in range(2):
                        bank = tt_ * 2 + half
                        DVE([PSK[bank], f"hb{b}"], [f"hb{b}"]).tensor_tensor(out=hb[b][:, half * 512:(half + 1) * 512], in0=ps[bank][:, :], in1=hb[b][:, half * 512:(half + 1) * 512], op=ALU.add)
                    ST(dst[i * 128:(i + 1) * 128, :], hb[b], f"hb{b}", ("hD", i))
                    if not last:
                        norm_tile(n, hb[b], f"hb{b}", i * 128, hnT, ("hnT", i), 4 + b)

            for tb in range(8):
                f_block(tb)
            S.barrier()

        phase_N1()
        for l in range(nlayers):
            if "H" in phases:
                phase_H(l)
            if "G" in phases:
                phase_G(l)
            if "A" in phases:
                phase_A(l)
            if "dbgOT" in dbg_aps:
                ST(dbg_aps["dbgOT"].rearrange("(m p) t -> p m t", p=128), OT, "OT", "dbgOT")
            if "dbgG" in dbg_aps:
                ST(dbg_aps["dbgG"].rearrange("(m p) t -> p m t", p=128), GT, "GT", "dbgG")
            if "dbgH" in dbg_aps:
                ST(dbg_aps["dbgH"].rearrange("(m p) t -> p m t", p=128), HT, "HT", "dbgH")
            if "dbg_hnT" in dbg_aps:
                S.dma("sp", dbg_aps["dbg_hnT"].rearrange("(k p) t -> p k t", p=128), hnT, "dbg1", reads=HN_ALL, writes=["dbg1"])
            S.barrier(keep=["wgall"])
            if "M" in phases:
                phase_M(l)
            if "dbg_h1" in dbg_aps and l == 0:
                S.dma("sp", dbg_aps["dbg_h1"], hbuf, "dbgh1", reads=[], writes=["dbgh1"])
                S.barrier()
            if "X" in phases:
                phase_X(l)
            if "dbg_h2" in dbg_aps and l == 0:
                S.dma("sp", dbg_aps["dbg_h2"], hbuf, "dbgh2", reads=[], writes=["dbgh2"])
                S.barrier(keep=["w1"])
            if "F" in phases:
                phase_F(l, l == nlayers - 1)
        S.barrier()
        S.emit(st)
    return nc


def make_in_maps(inputs):
    hc = host_consts()
    maps = []
    for c in range(8):
        m = {"x": np.ascontiguousarray(inputs["x"][c], dtype=np.float32),
             "mem": np.ascontiguousarray(inputs["mem"][c], dtype=np.float32)}
        for name, _ in WEIGHT_SPECS:
            m[name] = np.ascontiguousarray(inputs[name], dtype=np.float32)
        m.update(hc)
        maps.append(m)
    return maps


def kernel(**inputs):
    inputs = {k: np.asarray(v) for k, v in inputs.items()}
    nc = build()
    res = run_bass_kernel_spmd(nc, make_in_maps(inputs), core_ids=list(range(8)))
    return np.stack([np.asarray(r["out"], dtype=np.float32) for r in res.results], axis=0)
```
